# Optimizing a Trainium2 kernel written in Bass

```python
import jax, jax.numpy as jnp
from jax import lax
import numpy as np

D_MODEL = 1024
BATCH = 4
SEQ = 8192
DEPTH = 1

RWKV_HEADS = 8
RWKV_HEAD_DIM = 64
RWKV_WIDTH = RWKV_HEADS * RWKV_HEAD_DIM
DECAY_LORA = 64
ICLR_LORA = 64
GATE_LORA = 160
LNX_EPS = 64e-5
MOBA_HEADS = 8
MOBA_HEAD_DIM = 64
MOBA_WIDTH = MOBA_HEADS * MOBA_HEAD_DIM
MOBA_BLOCK = 256
MOBA_TOPK = 3
MOBA_Q_CHUNK = 64
N_BRANCH = 2
BRANCH_WIDTH = 512
RWKV_COLS = 3 * RWKV_WIDTH + DECAY_LORA + ICLR_LORA + GATE_LORA
MOBA_COLS = 3 * MOBA_WIDTH
GATE_COLS = N_BRANCH * D_MODEL
IN_COLS = RWKV_COLS + MOBA_COLS + GATE_COLS
RWKV_SPLITS = (RWKV_WIDTH, 2 * RWKV_WIDTH, 3 * RWKV_WIDTH,
               3 * RWKV_WIDTH + DECAY_LORA, 3 * RWKV_WIDTH + DECAY_LORA + ICLR_LORA)
PEER_HEADS = 8
PEER_N_KEYS = 128
PEER_N_EXPERTS = PEER_N_KEYS * PEER_N_KEYS
PEER_KEY_DIM = 128
PEER_TOPK = 16
PEER_TOKEN_CHUNK = 128

NORM_EPS = 1e-6
NEG_INF = -1e30

kernel_name = "rwkv7_moba_gated_peer_adaln_block"


def rms_norm(x, gain, eps=NORM_EPS):
    xf = x.astype(jnp.float32)
    y = xf * lax.rsqrt(jnp.mean(xf * xf, axis=-1, keepdims=True) + eps)
    return (y * gain.astype(jnp.float32)).astype(x.dtype)


def rwkv7_scan(r, w, k, v, a, b):
    B, S, H, N = r.shape

    def step(state, inp):
        r_t, w_t, k_t, v_t, a_t, b_t = inp
        sa = jnp.einsum('bhvk,bhk->bhv', state, a_t)
        state = (state * w_t[:, :, None, :] + sa[..., None] * b_t[:, :, None, :]
                 + v_t[..., None] * k_t[:, :, None, :])
        return state, jnp.einsum('bhvk,bhk->bhv', state, r_t)

    xs = tuple(jnp.swapaxes(t, 0, 1) for t in (r, w, k, v, a, b))
    state0 = jnp.zeros((B, H, N, N), jnp.float32)
    _, out = lax.scan(step, state0, xs)
    return jnp.swapaxes(out, 0, 1)


def rwkv7_branch(p, mu, w0, w2_decay, a0, a2_iclr, g2_gate, k_k, k_a, r_k, lnx_w, lnx_b):
    B, S, _ = p.shape
    H, N = RWKV_HEADS, RWKV_HEAD_DIM
    f32 = jnp.float32
    p_prev = jnp.pad(p, ((0, 0), (1, 0), (0, 0)))[:, :-1]
    xm = p + (p_prev - p) * mu
    r, k, v, wl, al, gl = jnp.split(xm, RWKV_SPLITS, axis=-1)
    w_log = -jax.nn.softplus(-(w0 + jnp.tanh(wl) @ w2_decay).astype(f32)) - 0.5
    decay = jnp.exp(-jnp.exp(w_log))
    a = jax.nn.sigmoid((a0 + al @ a2_iclr).astype(f32))
    g = jax.nn.sigmoid(gl) @ g2_gate
    r, k, v = r.astype(f32), k.astype(f32), v.astype(f32)

    def heads(t):
        return t.reshape(B, S, H, N)

    kk = heads(k * k_k)
    kk = kk / jnp.maximum(jnp.sqrt(jnp.sum(kk * kk, axis=-1, keepdims=True)), 1e-12)
    k = k * (1.0 + (a - 1.0) * k_a)
    out = rwkv7_scan(heads(r), heads(decay), heads(k), heads(v), -kk, kk * heads(a))
    mean = jnp.mean(out, axis=-1, keepdims=True)
    var = jnp.mean(jnp.square(out - mean), axis=-1, keepdims=True)
    out = ((out - mean) * lax.rsqrt(var + LNX_EPS)).reshape(B, S, RWKV_WIDTH) * lnx_w + lnx_b
    bonus = jnp.sum(heads(r) * heads(k) * r_k, axis=-1, keepdims=True) * heads(v)
    y = (out + bonus.reshape(B, S, RWKV_WIDTH)) * g
    return y.astype(p.dtype)


def moba_branch(q, k, v, q_norm_g, k_norm_g):
    B, S, _ = q.shape
    H, Dh, BL, QC = MOBA_HEADS, MOBA_HEAD_DIM, MOBA_BLOCK, MOBA_Q_CHUNK
    out_dtype = q.dtype

    def heads(t):
        return t.reshape(B, S, H, Dh).transpose(0, 2, 1, 3).astype(jnp.float32)

    q = rms_norm(heads(q), q_norm_g) * (Dh ** -0.5)
    k = rms_norm(heads(k), k_norm_g)
    v = heads(v)
    nb = -(-S // BL)
    pad = nb * BL - S
    kp = jnp.pad(k, ((0, 0), (0, 0), (0, pad), (0, 0)))
    vp = jnp.pad(v, ((0, 0), (0, 0), (0, pad), (0, 0)))
    kblk = kp.reshape(B, H, nb, BL, Dh)
    vblk = vp.reshape(B, H, nb, BL, Dh)
    kmean = jnp.mean(kblk, axis=3)
    n_sel = min(MOBA_TOPK, nb)
    qblk = jnp.arange(S) // BL
    gate = jnp.einsum('bhsd,bhnd->bhsn', q, kmean)
    gate = jnp.where(jnp.arange(nb)[None, :] < qblk[:, None], gate, NEG_INF)
    _, sel = lax.top_k(gate, n_sel)
    bi = jnp.arange(B)[:, None, None, None]
    hi = jnp.arange(H)[None, :, None, None]

    def chunk(ci):
        t0 = ci * QC
        blk = t0 // BL
        qc = lax.dynamic_slice_in_dim(q, t0, QC, axis=2)
        selc = lax.dynamic_slice_in_dim(sel, t0, QC, axis=2)
        kg = kblk[bi, hi, selc]
        vg = vblk[bi, hi, selc]
        k_own = lax.dynamic_slice_in_dim(kp, blk * BL, BL, axis=2)
        v_own = lax.dynamic_slice_in_dim(vp, blk * BL, BL, axis=2)
        s_sel = jnp.einsum('bhqd,bhqjkd->bhqjk', qc, kg)
        s_sel = jnp.where((jnp.arange(n_sel) < blk)[:, None], s_sel, NEG_INF)
        s_own = jnp.einsum('bhqd,bhkd->bhqk', qc, k_own)
        qpos = t0 + jnp.arange(QC)
        kpos = blk * BL + jnp.arange(BL)
        s_own = jnp.where(kpos[None, :] <= qpos[:, None], s_own, NEG_INF)
        scores = jnp.concatenate([s_sel.reshape(B, H, QC, n_sel * BL), s_own], axis=-1)
        p = jax.nn.softmax(scores, axis=-1)
        p_sel = p[..., :n_sel * BL].reshape(B, H, QC, n_sel, BL)
        p_own = p[..., n_sel * BL:]
        return (jnp.einsum('bhqjk,bhqjkd->bhqd', p_sel, vg)
                + jnp.einsum('bhqk,bhkd->bhqd', p_own, v_own))

    out = lax.map(chunk, jnp.arange(S // QC))
    out = out.transpose(1, 0, 3, 2, 4).reshape(B, S, H * Dh)
    return out.astype(out_dtype)


def peer_ffn(h, w_peer_q, sub_keys, u_tab, v_tab):
    B, S, D = h.shape
    HP, K = PEER_HEADS, PEER_TOPK
    q = (h @ w_peer_q).reshape(B, S, HP, 2, PEER_KEY_DIM).astype(jnp.float32)
    s1 = jnp.einsum('bshd,hnd->bshn', q[..., 0, :], sub_keys[0].astype(jnp.float32))
    s2 = jnp.einsum('bshd,hnd->bshn', q[..., 1, :], sub_keys[1].astype(jnp.float32))
    v1, i1 = lax.top_k(s1, K)
    v2, i2 = lax.top_k(s2, K)
    comb = (v1[..., :, None] + v2[..., None, :]).reshape(B, S, HP, K * K)
    sc, ci = lax.top_k(comb, K)
    e1 = jnp.take_along_axis(i1, ci // K, axis=-1)
    e2 = jnp.take_along_axis(i2, ci % K, axis=-1)
    idx = e1 * PEER_N_KEYS + e2
    gw = jax.nn.softmax(sc, axis=-1)
    n_ch = (B * S) // PEER_TOKEN_CHUNK
    hc = h.reshape(n_ch, PEER_TOKEN_CHUNK, D)
    idxc = idx.reshape(n_ch, PEER_TOKEN_CHUNK, HP * K)
    gwc = gw.reshape(n_ch, PEER_TOKEN_CHUNK, HP * K).astype(h.dtype)

    def chunk(args):
        h_c, i_c, g_c = args
        act = jax.nn.gelu(jnp.einsum('td,ted->te', h_c, u_tab[i_c]), approximate=False) * g_c
        return jnp.einsum('te,ted->td', act, v_tab[i_c])

    out = lax.map(chunk, (hc, idxc, gwc))
    return out.reshape(B, S, D)


def setup_inputs(seed: int = 0) -> dict:
    key = jax.random.key(seed)
    ks = jax.random.split(key, 32)
    nrm = jax.random.normal
    D = D_MODEL
    return {
        "x": nrm(ks[0], (BATCH, SEQ, D), jnp.float32),
        "c": nrm(ks[1], (BATCH, D), jnp.float32),
        "w_ada": nrm(ks[2], (D, 6 * D), jnp.float32) * (0.5 * D ** -0.5),
        "b_ada": nrm(ks[3], (6 * D,), jnp.float32) * 0.01,
        "norm1_g": 1.0 + 0.02 * nrm(ks[4], (D,), jnp.float32),
        "w_in": nrm(ks[5], (D, IN_COLS), jnp.float32) * D ** -0.5,
        "mu_rwkv": jax.random.uniform(ks[6], (RWKV_COLS,), jnp.float32),
        "w0": jax.random.uniform(ks[7], (RWKV_WIDTH,), jnp.float32, -6.0, 0.0),
        "w2_decay": nrm(ks[8], (DECAY_LORA, RWKV_WIDTH), jnp.float32) * DECAY_LORA ** -0.5,
        "a0": 0.1 * nrm(ks[9], (RWKV_WIDTH,), jnp.float32),
        "a2_iclr": nrm(ks[10], (ICLR_LORA, RWKV_WIDTH), jnp.float32) * ICLR_LORA ** -0.5,
        "g2_gate": nrm(ks[11], (GATE_LORA, RWKV_WIDTH), jnp.float32) * GATE_LORA ** -0.5,
        "k_k": 0.85 + 0.02 * nrm(ks[12], (RWKV_WIDTH,), jnp.float32),
        "k_a": 1.0 + 0.02 * nrm(ks[13], (RWKV_WIDTH,), jnp.float32),
        "r_k": 0.1 * nrm(ks[14], (RWKV_HEADS, RWKV_HEAD_DIM), jnp.float32),
        "lnx_w": 1.0 + 0.02 * nrm(ks[15], (RWKV_WIDTH,), jnp.float32),
        "lnx_b": 0.01 * nrm(ks[16], (RWKV_WIDTH,), jnp.float32),
        "q_norm_g": 1.0 + 0.02 * nrm(ks[17], (MOBA_HEAD_DIM,), jnp.float32),
        "k_norm_g": 1.0 + 0.02 * nrm(ks[18], (MOBA_HEAD_DIM,), jnp.float32),
        "w_branch": nrm(ks[19], (N_BRANCH, BRANCH_WIDTH, D), jnp.float32) * BRANCH_WIDTH ** -0.5,
        "w_out": nrm(ks[20], (D, D), jnp.float32) * D ** -0.5,
        "norm2_g": 1.0 + 0.02 * nrm(ks[21], (D,), jnp.float32),
        "w_peer_q": nrm(ks[22], (D, PEER_HEADS * 2 * PEER_KEY_DIM), jnp.float32) * D ** -0.5,
        "peer_sub_keys": nrm(ks[23], (2, PEER_HEADS, PEER_N_KEYS, PEER_KEY_DIM), jnp.float32) * PEER_KEY_DIM ** -0.5,
        "peer_u": nrm(ks[24], (PEER_N_EXPERTS, D), jnp.float32) * D ** -0.5,
        "peer_v": nrm(ks[25], (PEER_N_EXPERTS, D), jnp.float32) * (PEER_HEADS * PEER_TOPK) ** -0.5,
    }


def reference(x, c, w_ada, b_ada, norm1_g, w_in, mu_rwkv, w0, w2_decay, a0, a2_iclr, g2_gate,
              k_k, k_a, r_k, lnx_w, lnx_b, q_norm_g, k_norm_g, w_branch, w_out, norm2_g,
              w_peer_q, peer_sub_keys, peer_u, peer_v):
    B, S, D = x.shape
    for _ in range(DEPTH):
        ada = jax.nn.silu(c) @ w_ada + b_ada
        shift1, scale1, gate1, shift2, scale2, gate2 = jnp.split(ada[:, None, :], 6, axis=-1)
        h = rms_norm(x, norm1_g) * (1.0 + scale1) + shift1
        proj = h @ w_in
        p_rwkv, p_moba, p_gate = jnp.split(proj, (RWKV_COLS, RWKV_COLS + MOBA_COLS), axis=-1)
        y_a = rwkv7_branch(p_rwkv, mu_rwkv, w0, w2_decay, a0, a2_iclr, g2_gate,
                           k_k, k_a, r_k, lnx_w, lnx_b)
        q, k, v = jnp.split(p_moba, 3, axis=-1)
        y_b = moba_branch(q, k, v, q_norm_g, k_norm_g)
        ys = jnp.einsum('bsnc,ncd->bsnd', jnp.stack([y_a, y_b], axis=2), w_branch)
        gates = jax.nn.sigmoid(p_gate.reshape(B, S, N_BRANCH, D))
        mixed = jnp.sum(gates * ys, axis=2) @ w_out
        x = x + gate1 * mixed
        h2 = rms_norm(x, norm2_g) * (1.0 + scale2) + shift2
        x = x + gate2 * peer_ffn(h2, w_peer_q, peer_sub_keys, peer_u, peer_v)
    return x
```

```python
import numpy as np
from contextlib import ExitStack
import concourse.bass as bass
import concourse.mybir as mybir
from concourse.bass_utils import run_bass_kernel_spmd

F32 = mybir.dt.float32
BF16 = mybir.dt.bfloat16
U32 = mybir.dt.uint32
I32 = mybir.dt.int32
AF = mybir.ActivationFunctionType
ALU = mybir.AluOpType
AX = mybir.AxisListType

D = 1024
S_FULL = 8192
RW = 512
RWKV_COLS = 1824
IN_COLS = 5408
NEG = -1.0e30
LOCKSTEP = 0


class Res:
    __slots__ = ("name", "w", "r")

    def __init__(self, name):
        self.name = name
        self.w = None
        self.r = {}


class DSem:
    def __init__(self, sem):
        self.sem = sem
        self.count = 0


class Eng:
    def __init__(self, name, h, is_pe=False):
        self.name = name
        self.h = h
        self.is_pe = is_pe
        self.sem = None
        self.count = 0
        self.waited = {}

    def wait(self, dep):
        if dep is None:
            return
        sem, val, src = dep
        if src is self and self.is_pe:
            return
        if isinstance(src, DSem):
            val = max(val, src.count)
        key = sem.num
        if self.waited.get(key, 0) >= val:
            return
        self.h.wait_ge(sem, val)
        self.waited[key] = val

    def _pre(self, reads, writes):
        for r in reads:
            self.wait(r.w)
        for w in writes:
            self.wait(w.w)
            for d in w.r.values():
                self.wait(d)

    def _post(self, dep, reads, writes):
        for r in reads:
            old = r.r.get(dep[0].num)
            if old is None or old[1] < dep[1]:
                r.r[dep[0].num] = dep
        for w in writes:
            w.w = dep
            w.r = {}

    def op(self, fn, *args, reads=(), writes=(), signal=True, **kw):
        self._pre(reads, writes)
        ins = getattr(self.h, fn)(*args, **kw)
        if signal:
            self.count += 1
            ins.then_inc(self.sem, 1)
            val = self.count
        else:
            val = self.count + 1
        self._post((self.sem, val, self), reads, writes)
        return ins

    def dma(self, out, in_, dsem, reads=(), writes=(), **kw):
        self._pre(reads, writes)
        ins = self.h.dma_start(out=out, in_=in_, **kw)
        dsem.count += 16
        ins.then_inc(dsem.sem, 16)
        self._post((dsem.sem, dsem.count, dsem), reads, writes)
        return ins


class KB:
    def __init__(self, nc, es):
        self.nc = nc
        self.es = es
        self.pe = Eng("pe", nc.tensor, True)
        self.act = Eng("act", nc.scalar)
        self.dve = Eng("dve", nc.vector)
        self.pool = Eng("pool", nc.gpsimd)
        self.sp = Eng("sp", nc.sync)
        self.engs = [self.pe, self.act, self.dve, self.pool, self.sp]
        self.dsems = []
        self.phase = 0
        self._new_sems()
        self.nres = 0

    def _new_sems(self):
        for e in self.engs:
            e.sem = self.es.enter_context(self.nc.semaphore(f"s_{e.name}{self.phase}"))
            e.count = 0
            e.waited = {}

    def dsem(self, name):
        d = DSem(self.es.enter_context(self.nc.semaphore(name)))
        self.dsems.append(d)
        return d

    def res(self, name="r"):
        self.nres += 1
        return Res(f"{name}{self.nres}")

    def fence(self, d):
        for e in self.engs:
            if d.count > 0 and e.waited.get(d.sem.num, 0) < d.count:
                e.h.wait_ge(d.sem, d.count)
                e.waited[d.sem.num] = d.count

    def barrier(self):
        for e in self.engs:
            for o in self.engs:
                if o is not e and o.count > 0:
                    e.h.wait_ge(o.sem, o.count)
            for d in self.dsems:
                if d.count > 0:
                    e.h.wait_ge(d.sem, d.count)
        self.phase += 1
        self._new_sems()


class Tile:
    def __init__(self, kb, t, name):
        self.t = t
        self.res = kb.res(name)

    def __getitem__(self, k):
        return self.t[k]


def sb(kb, es, name, shape, dt):
    return Tile(kb, es.enter_context(kb.nc.sbuf_tensor(name, list(shape), dt)), name)


class _VA:
    def __init__(self, tile, i):
        self.tile = tile
        self.i = i
        self.res = tile.res

    def __getitem__(self, k):
        return self.tile.t[:, self.i, :][k]


def bc3(ap, shape):
    return ap.unsqueeze(2).to_broadcast(list(shape))


def build(nc, S=S_FULL, dbg=False, phases=("A", "B", "C", "D", "E"), halves=2):
    NT = S // 128
    NTE = NT // halves
    SE = S // halves
    kind_s = "ExternalOutput" if dbg else "Internal"
    dram = {}

    def din(name, shape, dt=F32):
        dram[name] = nc.dram_tensor(name, list(shape), dt, kind="ExternalInput").ap()
        return dram[name]

    def dscr(name, shape, dt=F32):
        dram[name] = nc.dram_tensor(name, list(shape), dt, kind=kind_s).ap()
        return dram[name]

    x = din("x", [S, D])
    c = din("c", [D])
    w_ada = din("w_ada", [D, 6 * D])
    b_ada = din("b_ada", [6 * D])
    norm1_g = din("norm1_g", [D])
    w_in = din("w_in", [D, IN_COLS])
    mu_rwkv = din("mu_rwkv", [RWKV_COLS])
    w0 = din("w0", [RW])
    w2_decay = din("w2_decay", [64, RW])
    a0 = din("a0", [RW])
    a2_iclr = din("a2_iclr", [64, RW])
    g2_gate = din("g2_gate", [160, RW])
    k_k = din("k_k", [RW])
    k_a = din("k_a", [RW])
    r_k = din("r_k", [RW])
    lnx_w = din("lnx_w", [RW])
    lnx_b = din("lnx_b", [RW])
    q_norm_g = din("q_norm_g", [64])
    k_norm_g = din("k_norm_g", [64])
    w_branch = din("w_branch", [2, 512, D])
    w_out = din("w_out", [D, D])
    norm2_g = din("norm2_g", [D])
    w_peer_q = din("w_peer_q", [D, 2048])
    peer_sub_keys = din("peer_sub_keys", [2, 8, 128, 128])
    peer_u = din("peer_u", [16384, D])
    peer_v = din("peer_v", [16384, D])
    cmask = din("cmask", [128, 2048])
    cneg = din("cneg", [2, 32 * 32])
    cb_mask = din("cb_mask", [128, 4, 512], BF16)
    emat = din("emat", [32, S], BF16)
    tokidx = din("tokidx", [128, NTE], I32)
    out = nc.dram_tensor("out", [SE, D], F32, kind="ExternalOutput").ap()
    if dbg:
        dbg_x1 = nc.dram_tensor("dbg_x1", [SE, D], F32, kind="ExternalOutput").ap()
        dbg_idx = nc.dram_tensor("dbg_idx", [SE, 128], I32, kind="ExternalOutput").ap()
        dbg_gw = nc.dram_tensor("dbg_gw", [SE, 128], F32, kind="ExternalOutput").ap()

    s_rw = dscr("s_rw", [S, 6, RW])
    s_rk = dscr("s_rk", [S, 8])
    s_g = dscr("s_g", [S, RW])
    s_qt = dscr("s_qt", [8, 64, S], BF16)
    s_kt = dscr("s_kt", [8, 64, S], BF16)
    s_va = dscr("s_va", [S, 8, 65], BF16)
    s_gates = dscr("s_gates", [S, 2048], BF16)
    s_ya = dscr("s_ya", [S, RW], BF16)
    s_yb = dscr("s_yb", [S, RW], BF16)
    s_ada = dscr("s_ada", [1, 6 * D])
    s_uv = nc.dram_tensor("s_uv", [16384, 2 * D], BF16, kind="Internal").ap()

    with ExitStack() as es:
        kb = KB(nc, es)
        pe, act, dve, pool, sp = kb.pe, kb.act, kb.dve, kb.pool, kb.sp
        banks = [Tile(kb, es.enter_context(nc.psum_tensor(f"bank{i}", [128, 512], F32)), f"bank{i}") for i in range(8)]

        ident_f = sb(kb, es, "ident_f", [128, 128], F32)
        ident_b = sb(kb, es, "ident_b", [128, 128], BF16)
        ones_f = sb(kb, es, "ones_f", [128, 128], F32)
        ada_col = sb(kb, es, "ada_col", [128, 48], F32)
        eps_t = sb(kb, es, "eps_t", [128, 1], F32)
        g1_col = sb(kb, es, "g1_col", [128, 8], F32)
        gs1_col = sb(kb, es, "gs1_col", [128, 8], F32)
        d_c = kb.dsem("d_const")

        d_o = kb.dsem("d_misc_out")
        sp.dma(ident_f[:], cmask[:, 0:128], d_c, writes=[ident_f.res])
        kb.fence(d_c)
        dve.op("tensor_copy", out=ident_b[:], in_=ident_f[:], reads=[ident_f.res], writes=[ident_b.res])
        dve.op("memset", ones_f[:], 1.0, writes=[ones_f.res])
        dve.op("memset", eps_t[:], 1e-6, writes=[eps_t.res])

        def bcast_row(dst, row_ap, ncols, row_res, bank):
            for j in range(0, ncols, 512):
                w = min(512, ncols - j)
                pe.op("matmul", bank[:, 0:w], lhsT=ones_f[0:1, 0:128], rhs=row_ap[0:1, j:j + w], start=True, stop=True,
                      reads=[ones_f.res, row_res], writes=[bank.res])
                act.op("copy", out=dst[:, j:j + w], in_=bank[:, 0:w], reads=[bank.res], writes=[dst.res])

        with ExitStack() as ea:
            ada_row = sb(kb, ea, "ada_row", [1, 6 * D], F32)
            c_col = sb(kb, ea, "c_col", [128, 8], F32)
            sc_col = sb(kb, ea, "sc_col", [128, 8], F32)
            bada_row = sb(kb, ea, "bada_row", [1, 6 * D], F32)
            wa = [sb(kb, ea, f"wa{i}", [128, 8, 512], F32) for i in range(2)]
            d_wa = [kb.dsem(f"d_wa{i}") for i in range(2)]
            sp.dma(c_col[:], c.rearrange("(k p) -> p k", p=128), d_c, writes=[c_col.res], allow_slow_non_contiguous=True)
            sp.dma(g1_col[:], norm1_g.rearrange("(k p) -> p k", p=128), d_c, writes=[g1_col.res], allow_slow_non_contiguous=True)
            sp.dma(bada_row[:], b_ada.rearrange("(o n) -> o n", o=1), d_c, writes=[bada_row.res])
            kb.fence(d_c)
            act.op("activation", out=sc_col[:], in_=c_col[:], func=AF.Silu, reads=[c_col.res], writes=[sc_col.res])
            wa_v = w_ada.rearrange("(k p) n -> p k n", p=128)
            for cb in range(12):
                wt = wa[cb % 2]
                sp.dma(wt[:], wa_v[:, :, cb * 512:(cb + 1) * 512], d_wa[cb % 2], writes=[wt.res])
                bk = banks[cb % 2]
                for k in range(8):
                    pe.op("matmul", bk[0:1, 0:512], lhsT=sc_col[:, k:k + 1], rhs=wt[:, k, :], start=(k == 0), stop=(k == 7),
                          reads=[sc_col.res, wt.res], writes=[bk.res], signal=(k == 7))
                dve.op("tensor_tensor", out=ada_row[0:1, cb * 512:(cb + 1) * 512], in0=bk[0:1, 0:512],
                       in1=bada_row[0:1, cb * 512:(cb + 1) * 512], op=ALU.add,
                       reads=[bk.res, bada_row.res], writes=[ada_row.res])
            bk = banks[2]
            for m in range(48):
                pe.op("matmul", bk[:, m:m + 1], lhsT=ada_row[0:1, m * 128:(m + 1) * 128], rhs=ones_f[0:1, 0:1],
                      start=True, stop=True, reads=[ada_row.res, ones_f.res], writes=[bk.res], signal=(m == 47))
            dve.op("tensor_copy", out=ada_col[:], in_=bk[:, 0:48], reads=[bk.res], writes=[ada_col.res])
            dve.op("scalar_tensor_tensor", out=gs1_col[:], in0=ada_col[:, 8:16], scalar=1.0, in1=g1_col[:], op0=ALU.add, op1=ALU.mult,
                   reads=[ada_col.res, g1_col.res], writes=[gs1_col.res])
            sp.dma(s_ada[0:1, :], ada_row[:], d_o, reads=[ada_row.res])
            kb.barrier()


        if "B" in phases:
          with ExitStack() as eb:
            Wp = sb(kb, eb, "Wp", [128, 8, 7232], BF16)
            brow = sb(kb, eb, "brow", [1, 7232], BF16)
            with ExitStack() as eb0:
                bprime = sb(kb, eb0, "bprime", [1, IN_COLS], F32)
                mu_bc = sb(kb, eb0, "mu_bc", [128, RWKV_COLS], F32)
                om_bc = sb(kb, eb0, "om_bc", [128, RWKV_COLS], F32)
                w32 = [sb(kb, eb0, f"w32_{i}", [128, 8, 256], F32) for i in range(2)]
                d_w32 = [kb.dsem(f"d_w32_{i}") for i in range(2)]
                sp.dma(mu_bc[:], mu_rwkv.partition_broadcast(128), d_c, writes=[mu_bc.res])
                kb.fence(d_c)
                dve.op("tensor_scalar", out=om_bc[:], in0=mu_bc[:], scalar1=-1.0, scalar2=1.0, op0=ALU.mult, op1=ALU.add,
                       reads=[mu_bc.res], writes=[om_bc.res])
                win_v = w_in.rearrange("(k p) n -> p k n", p=128)
                for cb in range(22):
                    c0 = cb * 256
                    c1 = min(IN_COLS, c0 + 256)
                    w = c1 - c0
                    wt = w32[cb % 2]
                    sp.dma(wt[:, :, 0:w], win_v[:, :, c0:c1], d_w32[cb % 2], writes=[wt.res])
                    bk = banks[cb % 2]
                    for k in range(8):
                        pe.op("matmul", bk[0:1, 0:w], lhsT=ada_col[:, k:k + 1], rhs=wt[:, k, 0:w], start=(k == 0), stop=(k == 7),
                              reads=[ada_col.res, wt.res], writes=[bk.res], signal=(k == 7))
                    dve.op("tensor_copy", out=bprime[0:1, c0:c1], in_=bk[0:1, 0:w], reads=[bk.res], writes=[bprime.res])
                    ra, rb = c0, min(c1, RWKV_COLS)
                    for k in range(8):
                        if rb > ra:
                            dve.op("scalar_tensor_tensor", out=Wp[:, k, ra:rb], in0=wt[:, k, ra - c0:rb - c0], scalar=gs1_col[:, k:k + 1],
                                   in1=om_bc[:, ra:rb], op0=ALU.mult, op1=ALU.mult, reads=[wt.res, gs1_col.res, om_bc.res], writes=[Wp.res])
                            dve.op("scalar_tensor_tensor", out=Wp[:, k, RWKV_COLS + ra:RWKV_COLS + rb], in0=wt[:, k, ra - c0:rb - c0],
                                   scalar=gs1_col[:, k:k + 1], in1=mu_bc[:, ra:rb], op0=ALU.mult, op1=ALU.mult,
                                   reads=[wt.res, gs1_col.res, mu_bc.res], writes=[Wp.res])
                        qa, qb = max(c0, RWKV_COLS), c1
                        if qb > qa:
                            dve.op("tensor_scalar", out=Wp[:, k, RWKV_COLS + qa:RWKV_COLS + qb], in0=wt[:, k, qa - c0:qb - c0],
                                   scalar1=gs1_col[:, k:k + 1], scalar2=None, op0=ALU.mult, reads=[wt.res, gs1_col.res], writes=[Wp.res])
                dve.op("tensor_tensor", out=brow[0:1, 0:RWKV_COLS], in0=bprime[0:1, 0:RWKV_COLS], in1=om_bc[0:1, :], op=ALU.mult,
                       reads=[bprime.res, om_bc.res], writes=[brow.res])
                dve.op("tensor_tensor", out=brow[0:1, RWKV_COLS:2 * RWKV_COLS], in0=bprime[0:1, 0:RWKV_COLS], in1=mu_bc[0:1, :], op=ALU.mult,
                       reads=[bprime.res, mu_bc.res], writes=[brow.res])
                dve.op("tensor_copy", out=brow[0:1, 2 * RWKV_COLS:7232], in_=bprime[0:1, RWKV_COLS:IN_COLS], reads=[bprime.res], writes=[brow.res])
                kb.barrier()

            xt = [sb(kb, eb, f"xt{i}", [128, D], F32) for i in range(2)]
            d_x = [kb.dsem(f"d_x{i}") for i in range(2)]
            junk = sb(kb, eb, "junk", [128, D], BF16)
            ss = sb(kb, eb, "ss", [128, 1], F32)
            rstd = sb(kb, eb, "rstd", [128, 1], F32)
            xn = sb(kb, eb, "xn", [128, D], BF16)
            xnT = [sb(kb, eb, f"xnT{i}", [128, 8, 129], BF16) for i in range(2)]
            ones_a = sb(kb, eb, "ones_a", [1, 129], BF16)
            ones_b = sb(kb, eb, "ones_b", [1, 129], BF16)
            kk_bc = sb(kb, eb, "kk_bc", [128, RW], F32)
            ka_bc = sb(kb, eb, "ka_bc", [128, RW], F32)
            rk_bc = sb(kb, eb, "rk_bc", [128, RW], F32)
            qg_bc = sb(kb, eb, "qg_bc", [128, 8, 64], F32)
            kg_bc = sb(kb, eb, "kg_bc", [128, 8, 64], F32)
            w0a0 = sb(kb, eb, "w0a0", [1, 2 * RW], F32)
            w2d = sb(kb, eb, "w2d", [64, RW], F32)
            a2i = sb(kb, eb, "a2i", [64, RW], F32)
            g2g = sb(kb, eb, "g2g", [128, 2, RW], F32)
            tw = sb(kb, eb, "tw", [64, 128], F32)
            alT = sb(kb, eb, "alT", [64, 128], F32)
            sg0 = sb(kb, eb, "sg0", [128, 128], F32)
            sg1 = sb(kb, eb, "sg1", [32, 128], F32)
            a_s = sb(kb, eb, "a_s", [128, RW], F32)
            g_s = [sb(kb, eb, f"g_s{i}", [128, RW], F32) for i in range(1)]
            kraw = sb(kb, eb, "kraw", [128, RW], F32)
            t1 = sb(kb, eb, "t1", [128, RW], F32)
            t2 = sb(kb, eb, "t2", [128, RW], F32)
            t3 = sb(kb, eb, "t3", [128, RW], F32)
            ssq = sb(kb, eb, "ssq", [128, 8], F32)
            rinv = sb(kb, eb, "rinv", [128, 8], F32)
            rw_out = [sb(kb, eb, f"rw_out{i}", [128, 6, RW], F32) for i in range(1)]
            rk_out = [sb(kb, eb, f"rk_out{i}", [128, 8], F32) for i in range(2)]
            qm = sb(kb, eb, "qm", [128, RW], F32)
            qn = sb(kb, eb, "qn", [128, RW], BF16)
            kn = sb(kb, eb, "kn", [128, RW], BF16)
            va = [sb(kb, eb, f"va{i}", [128, 8, 65], BF16) for i in range(2)]
            qT_s = [sb(kb, eb, f"qT_s{i}", [128, 4, 128], BF16) for i in range(2)]
            kT_s = [sb(kb, eb, f"kT_s{i}", [128, 4, 128], BF16) for i in range(2)]
            gates_o = [sb(kb, eb, f"gates_o{i}", [128, 2048], BF16) for i in range(1)]
            d_st = [kb.dsem(f"d_stB{i}") for i in range(2)]
            eps64 = sb(kb, eb, "eps64", [128, 1], F32)

            sp.dma(kk_bc[:], k_k.partition_broadcast(128), d_c, writes=[kk_bc.res])
            sp.dma(ka_bc[:], k_a.partition_broadcast(128), d_c, writes=[ka_bc.res])
            sp.dma(rk_bc[:], r_k.partition_broadcast(128), d_c, writes=[rk_bc.res])
            for h in range(8):
                sp.dma(qg_bc[:, h, :], q_norm_g.partition_broadcast(128), d_c, writes=[qg_bc.res])
                sp.dma(kg_bc[:, h, :], k_norm_g.partition_broadcast(128), d_c, writes=[kg_bc.res])
            sp.dma(w0a0[0:1, 0:RW], w0.rearrange("(o n) -> o n", o=1), d_c, writes=[w0a0.res])
            sp.dma(w0a0[0:1, RW:2 * RW], a0.rearrange("(o n) -> o n", o=1), d_c, writes=[w0a0.res])
            sp.dma(w2d[:], w2_decay[:, :], d_c, writes=[w2d.res])
            sp.dma(a2i[:], a2_iclr[:, :], d_c, writes=[a2i.res])
            sp.dma(g2g[:, 0, :], g2_gate[0:128, :], d_c, writes=[g2g.res])
            sp.dma(g2g[0:32, 1, :], g2_gate[128:160, :], d_c, writes=[g2g.res])
            kb.fence(d_c)
            dve.op("tensor_scalar", out=qg_bc[:], in0=qg_bc[:], scalar1=0.125, scalar2=None, op0=ALU.mult, reads=[qg_bc.res], writes=[qg_bc.res])
            dve.op("memset", ones_a[:], 1.0, writes=[ones_a.res])
            dve.op("memset", ones_b[:], 1.0, writes=[ones_b.res])
            dve.op("memset", ones_b[0:1, 0:1], 0.0, writes=[ones_b.res])
            dve.op("memset", eps64[:], 1e-6, writes=[eps64.res])
            for i in range(2):
                dve.op("memset", va[i][:, :, 64:65], 1.0, writes=[va[i].res])
            bankT = banks[0]
            bT = bankT[:].bitcast(BF16)

            def head_rstd(src_ap, src_res, scale, eps_tile):
                dve.op("tensor_tensor", out=t2[:], in0=src_ap, in1=src_ap, op=ALU.mult, reads=[src_res], writes=[t2.res])
                dve.op("tensor_reduce", out=ssq[:], in_=t2[:].rearrange("p (h d) -> p h d", h=8), axis=AX.X, op=ALU.add,
                       reads=[t2.res], writes=[ssq.res])
                if eps_tile is None:
                    act.op("activation", out=ssq[:], in_=ssq[:], func=AF.Sqrt, scale=scale, reads=[ssq.res], writes=[ssq.res])
                    dve.op("tensor_scalar", out=ssq[:], in0=ssq[:], scalar1=1e-12, scalar2=None, op0=ALU.max, reads=[ssq.res], writes=[ssq.res])
                else:
                    act.op("activation", out=ssq[:], in_=ssq[:], func=AF.Sqrt, scale=scale, bias=eps_tile[:, 0:1],
                           reads=[ssq.res, eps_tile.res], writes=[ssq.res])
                dve.op("reciprocal", out=rinv[:], in_=ssq[:], reads=[ssq.res], writes=[rinv.res])

            def load_x(tt):
                sp.dma(xt[tt % 2][:], x[tt * 128:(tt + 1) * 128, :], d_x[tt % 2], writes=[xt[tt % 2].res])

            load_x(0)
            for tt in range(NT):
                if tt + 1 < NT:
                    load_x(tt + 1)
                xc = xt[tt % 2]
                xT = xnT[tt % 2]
                xTp = xnT[(tt + 1) % 2]
                ro = rw_out[0]
                rko = rk_out[tt % 2]
                gs_ = g_s[0]
                tsl = slice(tt * 128, (tt + 1) * 128)
                dst = d_st[tt % 2]
                act.op("activation", out=junk[:], in_=xc[:], func=AF.Square, accum_out=ss[:], reads=[xc.res], writes=[junk.res, ss.res])
                act.op("activation", out=rstd[:], in_=ss[:], func=AF.Sqrt, scale=1.0 / D, bias=eps_t[:, 0:1],
                       reads=[ss.res, eps_t.res], writes=[rstd.res])
                dve.op("reciprocal", out=rstd[:], in_=rstd[:], reads=[rstd.res], writes=[rstd.res])
                act.op("activation", out=xn[:], in_=xc[:], func=AF.Copy, scale=rstd[:, 0:1], reads=[xc.res, rstd.res], writes=[xn.res])
                for k in range(8):
                    pe.op("transpose", out=bT[:, k * 128:(k + 1) * 128], in_=xn[:, k * 128:(k + 1) * 128], identity=ident_b[:],
                          reads=[xn.res, ident_b.res], writes=[bankT.res], signal=(k == 7))
                dve.op("tensor_copy", out=xT[:, :, 1:129], in_=bT[:, :].rearrange("p (k t) -> p k t", k=8), reads=[bankT.res], writes=[xT.res])
                if tt == 0:
                    dve.op("memset", xT[:, :, 0:1], 0.0, writes=[xT.res])
                else:
                    dve.op("tensor_copy", out=xT[:, :, 0:1], in_=xTp[:, :, 128:129], reads=[xTp.res], writes=[xT.res])
                osh = ones_b if tt == 0 else ones_a

                def proj_tm(bk, w, cA, shifted):
                    n = 17 if shifted else 8
                    i = 0
                    for k in range(8):
                        pe.op("matmul", bk[:, 0:w], lhsT=xT[:, k, 1:129], rhs=Wp[:, k, cA:cA + w], start=(i == 0), stop=False,
                              reads=[xT.res, Wp.res], writes=[bk.res], signal=False)
                        i += 1
                    if shifted:
                        for k in range(8):
                            pe.op("matmul", bk[:, 0:w], lhsT=xT[:, k, 0:128], rhs=Wp[:, k, RWKV_COLS + cA:RWKV_COLS + cA + w], start=False, stop=False,
                                  reads=[xT.res, Wp.res], writes=[bk.res], signal=False)
                        pe.op("matmul", bk[:, 0:w], lhsT=osh[0:1, 0:128], rhs=brow[0:1, RWKV_COLS + cA:RWKV_COLS + cA + w], start=False, stop=False,
                              reads=[osh.res, brow.res], writes=[bk.res], signal=False)
                    pe.op("matmul", bk[:, 0:w], lhsT=ones_a[0:1, 1:129], rhs=brow[0:1, cA:cA + w], start=False, stop=True,
                          reads=[ones_a.res, brow.res], writes=[bk.res], signal=True)

                def proj_fm(bk, col0, M, cA):
                    o = bk[0:M, col0:col0 + 128]
                    for k in range(8):
                        pe.op("matmul", o, lhsT=Wp[:, k, cA:cA + M], rhs=xT[:, k, 1:129], start=(k == 0), stop=False,
                              reads=[xT.res, Wp.res], writes=[bk.res], signal=False)
                    for k in range(8):
                        pe.op("matmul", o, lhsT=Wp[:, k, RWKV_COLS + cA:RWKV_COLS + cA + M], rhs=xT[:, k, 0:128], start=False, stop=False,
                              reads=[xT.res, Wp.res], writes=[bk.res], signal=False)
                    pe.op("matmul", o, lhsT=brow[0:1, RWKV_COLS + cA:RWKV_COLS + cA + M], rhs=osh[0:1, 0:128], start=False, stop=False,
                          reads=[osh.res, brow.res], writes=[bk.res], signal=False)
                    pe.op("matmul", o, lhsT=brow[0:1, cA:cA + M], rhs=ones_a[0:1, 1:129], start=False, stop=True,
                          reads=[ones_a.res, brow.res], writes=[bk.res], signal=True)

                b4 = banks[4]
                proj_fm(b4, 0, 64, 1536)
                proj_fm(b4, 128, 64, 1600)
                proj_fm(b4, 256, 128, 1664)
                proj_fm(b4, 384, 32, 1792)
                act.op("activation", out=tw[:], in_=b4[0:64, 0:128], func=AF.Tanh, reads=[b4.res], writes=[tw.res])
                act.op("activation", out=sg0[:], in_=b4[:, 256:384], func=AF.Sigmoid, reads=[b4.res], writes=[sg0.res])
                act.op("activation", out=sg1[:], in_=b4[0:32, 384:512], func=AF.Sigmoid, reads=[b4.res], writes=[sg1.res])
                dve.op("tensor_copy", out=alT[:], in_=b4[0:64, 128:256], reads=[b4.res], writes=[alT.res])
                proj_tm(banks[1], 512, 0, True)
                proj_tm(banks[2], 512, 512, True)
                proj_tm(banks[3], 512, 1024, True)
                pe.op("matmul", banks[5][:, :], lhsT=tw[:, :], rhs=w2d[:, :], start=True, stop=False, reads=[tw.res, w2d.res], writes=[banks[5].res], signal=False)
                pe.op("matmul", banks[5][:, :], lhsT=ones_f[0:1, 0:128], rhs=w0a0[0:1, 0:RW], start=False, stop=True,
                      reads=[ones_f.res, w0a0.res], writes=[banks[5].res])
                pe.op("matmul", banks[6][:, :], lhsT=alT[:, :], rhs=a2i[:, :], start=True, stop=False, reads=[alT.res, a2i.res], writes=[banks[6].res], signal=False)
                pe.op("matmul", banks[6][:, :], lhsT=ones_f[0:1, 0:128], rhs=w0a0[0:1, RW:2 * RW], start=False, stop=True,
                      reads=[ones_f.res, w0a0.res], writes=[banks[6].res])
                pe.op("matmul", banks[7][:, :], lhsT=sg0[:, :], rhs=g2g[:, 0, :], start=True, stop=False, reads=[sg0.res, g2g.res], writes=[banks[7].res], signal=False)
                pe.op("matmul", banks[7][:, :], lhsT=sg1[:, :], rhs=g2g[0:32, 1, :], start=False, stop=True, reads=[sg1.res, g2g.res], writes=[banks[7].res])
                act.op("copy", out=ro[:, 0, :], in_=banks[1][:, :], reads=[banks[1].res], writes=[ro.res])
                act.op("copy", out=kraw[:], in_=banks[2][:, :], reads=[banks[2].res], writes=[kraw.res])
                act.op("copy", out=ro[:, 3, :], in_=banks[3][:, :], reads=[banks[3].res], writes=[ro.res])
                act.op("activation", out=t3[:], in_=banks[5][:, :], func=AF.Sigmoid, reads=[banks[5].res], writes=[t3.res])
                dve.op("tensor_scalar", out=ro[:, 1, :], in0=t3[:], scalar1=-0.6065306597126334, scalar2=None, op0=ALU.mult, reads=[t3.res], writes=[ro.res])
                act.op("activation", out=a_s[:], in_=banks[6][:, :], func=AF.Sigmoid, reads=[banks[6].res], writes=[a_s.res])
                act.op("copy", out=gs_[:], in_=banks[7][:, :], reads=[banks[7].res], writes=[gs_.res])
                dve.op("tensor_tensor", out=t1[:], in0=kraw[:], in1=kk_bc[:], op=ALU.mult, reads=[kraw.res, kk_bc.res], writes=[t1.res])
                head_rstd(t1[:], t1.res, 1.0, None)
                dve.op("tensor_tensor", out=ro[:, 4, :].rearrange("p (h d) -> p h d", h=8), in0=t1[:].rearrange("p (h d) -> p h d", h=8),
                       in1=bc3(rinv[:], [128, 8, 64]), op=ALU.mult, reads=[t1.res, rinv.res], writes=[ro.res])
                dve.op("scalar_tensor_tensor", out=t2[:], in0=a_s[:], scalar=-1.0, in1=ka_bc[:], op0=ALU.add, op1=ALU.mult,
                       reads=[a_s.res, ka_bc.res], writes=[t2.res])
                dve.op("scalar_tensor_tensor", out=ro[:, 2, :], in0=t2[:], scalar=1.0, in1=kraw[:], op0=ALU.add, op1=ALU.mult,
                       reads=[t2.res, kraw.res], writes=[ro.res])
                dve.op("tensor_tensor", out=ro[:, 5, :], in0=ro[:, 4, :], in1=a_s[:], op=ALU.mult, reads=[ro.res, a_s.res], writes=[ro.res])
                dve.op("tensor_tensor", out=t1[:], in0=ro[:, 0, :], in1=ro[:, 2, :], op=ALU.mult, reads=[ro.res], writes=[t1.res])
                dve.op("tensor_tensor", out=t1[:], in0=t1[:], in1=rk_bc[:], op=ALU.mult, reads=[t1.res, rk_bc.res], writes=[t1.res])
                dve.op("tensor_reduce", out=rko[:], in_=t1[:].rearrange("p (h d) -> p h d", h=8), axis=AX.X, op=ALU.add, reads=[t1.res], writes=[rko.res])
                sp.dma(s_rw[tsl, :, :], ro[:], dst, reads=[ro.res])
                sp.dma(s_rk[tsl, :], rko[:], dst, reads=[rko.res])
                sp.dma(s_g[tsl, :], gs_[:], dst, reads=[gs_.res])
                vo = va[tt % 2]
                qTs = qT_s[tt % 2]
                kTs = kT_s[tt % 2]
                proj_tm(banks[1], 512, 2 * RWKV_COLS, False)
                proj_tm(banks[2], 512, 2 * RWKV_COLS + 512, False)
                proj_tm(banks[3], 512, 2 * RWKV_COLS + 1024, False)
                act.op("copy", out=vo[:, :, 0:64], in_=banks[3][:, :].rearrange("p (h d) -> p h d", h=8), reads=[banks[3].res], writes=[vo.res])
                for (bk, gb, dstn, dT, sT) in ((banks[1], qg_bc, qn, qTs, s_qt), (banks[2], kg_bc, kn, kTs, s_kt)):
                    act.op("copy", out=qm[:], in_=bk[:, :], reads=[bk.res], writes=[qm.res])
                    head_rstd(qm[:], qm.res, 1.0 / 64, eps64)
                    dve.op("tensor_tensor", out=t1[:].rearrange("p (h d) -> p h d", h=8), in0=qm[:].rearrange("p (h d) -> p h d", h=8),
                           in1=bc3(rinv[:], [128, 8, 64]), op=ALU.mult, reads=[qm.res, rinv.res], writes=[t1.res])
                    dve.op("tensor_tensor", out=dstn[:], in0=t1[:], in1=gb[:].rearrange("p h d -> p (h d)"), op=ALU.mult,
                           reads=[t1.res, gb.res], writes=[dstn.res])
                    for j in range(4):
                        pe.op("transpose", out=bT[:, j * 128:(j + 1) * 128], in_=dstn[:, j * 128:(j + 1) * 128], identity=ident_b[:],
                              reads=[dstn.res, ident_b.res], writes=[bankT.res], signal=(j == 3))
                    dve.op("tensor_copy", out=dT[:], in_=bT[:, 0:512].rearrange("p (j t) -> p j t", j=4), reads=[bankT.res], writes=[dT.res])
                    sp.dma(sT[:, :, tsl].rearrange("(j h2) d t -> (h2 d) j t", h2=2), dT[:], dst, reads=[dT.res])
                sp.dma(s_va[tsl, :, :], vo[:], dst, reads=[vo.res])
                go = gates_o[0]
                for j in range(4):
                    bk = banks[5 + (j % 3)]
                    proj_tm(bk, 512, 2 * RWKV_COLS + 1536 + j * 512, False)
                    act.op("activation", out=go[:, j * 512:(j + 1) * 512], in_=bk[:, :], func=AF.Sigmoid, reads=[bk.res], writes=[go.res])
                sp.dma(s_gates[tsl, :], go[:], dst, reads=[go.res])
            kb.barrier()


        if "C" in phases:
          with ExitStack() as ec:
            NCH = S // 64
            NIB = 4
            rr = [0]

            def nb():
                rr[0] = (rr[0] + 1) % 8
                return banks[rr[0]]

            def t64(name, dt=F32):
                return sb(kb, ec, name, [64, RW], dt)

            SLm, SLTm, LETm, I8 = t64("SLm"), t64("SLTm"), t64("LETm"), t64("I8")
            Ltri = sb(kb, ec, "Ltri", [64, 64], F32)
            lnw_bc, lnb_bc = t64("lnw_bc"), t64("lnb_bc")
            eps_ln = sb(kb, ec, "eps_ln", [64, 1], F32)
            ST = t64("ST")
            inp = [sb(kb, ec, f"c_in{i}", [64, 6, RW], F32) for i in range(NIB)]
            gin = [t64(f"c_g{i}") for i in range(NIB)]
            rkin = [sb(kb, ec, f"c_rk{i}", [64, 8], F32) for i in range(NIB)]
            d_ci = [kb.dsem(f"d_ci{i}") for i in range(NIB)]
            d_co = [kb.dsem(f"d_co{i}") for i in range(2)]
            ya_o = [t64(f"ya_o{i}", BF16) for i in range(2)]

            class Ctx:
                pass

            ctxs = []
            for ci_ in range(2):
                c_ = Ctx()
                sfx = f"_{ci_}"
                for nm in ("cum_s", "cump", "tmc", "e_cum", "e_ncum", "e_cump", "e_tc", "e_tot", "At", "Bt", "Kt", "Rt", "Bh", "Kh", "Dg",
                           "AtT", "BtT", "KtT", "RtT", "Pm0", "Pm1", "PTm0", "PTm1", "MakT", "NrbT", "NrkT", "Q0", "Q1"):
                    setattr(c_, nm, t64(nm + sfx))
                c_.mean8 = sb(kb, ec, "mean8" + sfx, [64, 8], F32)
                c_.var8 = sb(kb, ec, "var8" + sfx, [64, 8], F32)
                c_.X2, c_.What, c_.Ahat, c_.PhiT, c_.RhT, c_.Osb, c_.cen, c_.sqv = (c_.cum_s, c_.cump, c_.tmc, c_.e_cum, c_.e_ncum, c_.e_cump, c_.e_tc, c_.e_tot)
                ctxs.append(c_)

            sp.dma(SLm[:], cmask[0:64, 512:1024], d_c, writes=[SLm.res])
            sp.dma(SLTm[:], cmask[0:64, 1024:1536], d_c, writes=[SLTm.res])
            sp.dma(LETm[:], cmask[0:64, 1536:2048], d_c, writes=[LETm.res])
            sp.dma(Ltri[:], cmask[0:64, 128:192], d_c, writes=[Ltri.res])
            sp.dma(lnw_bc[:], lnx_w.partition_broadcast(64), d_c, writes=[lnw_bc.res])
            sp.dma(lnb_bc[:], lnx_b.partition_broadcast(64), d_c, writes=[lnb_bc.res])
            kb.fence(d_c)
            for h in range(8):
                dve.op("tensor_copy", out=I8[:, h * 64:(h + 1) * 64], in_=ident_f[0:64, 0:64], reads=[ident_f.res], writes=[I8.res])
            dve.op("memset", ST[:], 0.0, writes=[ST.res])
            dve.op("memset", eps_ln[:], 64e-5, writes=[eps_ln.res])
            I64 = ident_f[0:64, 0:64]

            def hs(h):
                return slice(h * 64, (h + 1) * 64)

            def v3(ap):
                return ap.rearrange("p (h d) -> p h d", h=8)

            def load_c(ci):
                b_ = ci % NIB
                tsl = slice(ci * 64, (ci + 1) * 64)
                sp.dma(inp[b_][:], s_rw[tsl, :, :], d_ci[b_], writes=[inp[b_].res])
                sp.dma(gin[b_][:], s_g[tsl, :], d_ci[b_], writes=[gin[b_].res])
                sp.dma(rkin[b_][:], s_rk[tsl, :], d_ci[b_], writes=[rkin[b_].res])

            def mm_heads(bk, lhs, rhs, extra=None, signal_last=True):
                for h in range(8):
                    terms = [(lhs, rhs)] + (extra or [])
                    for ti, (l_, r_) in enumerate(terms):
                        l_ap = l_[0][:, hs(h)] if isinstance(l_, tuple) else I64
                        l_res = l_[0].res if isinstance(l_, tuple) else ident_f.res
                        pe.op("matmul", bk[0:64, hs(h)], lhsT=l_ap, rhs=r_[:, hs(h)], start=(ti == 0), stop=(ti == len(terms) - 1),
                              reads=[l_res, r_.res], writes=[bk.res], signal=(signal_last and h == 7 and ti == len(terms) - 1))

            def chunk_gen(ci, c):
                b_ = ci % NIB
                X = inp[b_]
                tsl = slice(ci * 64, (ci + 1) * 64)
                r_, ld_, k2_, v_, kkn_, bb_ = (X[:, i, :] for i in range(6))
                VV = _VA(X, 3)
                bcum, btot = nb(), nb()
                pe.op("matmul", bcum[0:64, :], lhsT=Ltri[:, :], rhs=ld_, start=True, stop=True, reads=[Ltri.res, X.res], writes=[bcum.res])
                pe.op("matmul", btot[0:64, :], lhsT=ones_f[0:64, 0:64], rhs=ld_, start=True, stop=True, reads=[ones_f.res, X.res], writes=[btot.res])
                yield
                act.op("copy", out=c.cum_s[:], in_=bcum[0:64, :], reads=[bcum.res], writes=[c.cum_s.res])
                act.op("activation", out=c.e_tot[:], in_=btot[0:64, :], func=AF.Exp, reads=[btot.res], writes=[c.e_tot.res])
                yield
                dve.op("tensor_tensor", out=c.cump[:], in0=c.cum_s[:], in1=ld_, op=ALU.subtract, reads=[c.cum_s.res, X.res], writes=[c.cump.res])
                dve.op("tensor_tensor", out=c.tmc[:], in0=btot[0:64, :], in1=c.cum_s[:], op=ALU.subtract, reads=[btot.res, c.cum_s.res], writes=[c.tmc.res])
                act.op("activation", out=c.e_cum[:], in_=c.cum_s[:], func=AF.Exp, reads=[c.cum_s.res], writes=[c.e_cum.res])
                act.op("activation", out=c.e_ncum[:], in_=c.cum_s[:], func=AF.Exp, scale=-1.0, reads=[c.cum_s.res], writes=[c.e_ncum.res])
                yield
                act.op("activation", out=c.e_cump[:], in_=c.cump[:], func=AF.Exp, reads=[c.cump.res], writes=[c.e_cump.res])
                act.op("activation", out=c.e_tc[:], in_=c.tmc[:], func=AF.Exp, reads=[c.tmc.res], writes=[c.e_tc.res])
                dve.op("tensor_tensor", out=c.Bt[:], in0=bb_, in1=c.e_ncum[:], op=ALU.mult, reads=[X.res, c.e_ncum.res], writes=[c.Bt.res])
                dve.op("tensor_tensor", out=c.Kt[:], in0=k2_, in1=c.e_ncum[:], op=ALU.mult, reads=[X.res, c.e_ncum.res], writes=[c.Kt.res])
                dve.op("tensor_tensor", out=c.Rt[:], in0=r_, in1=c.e_cum[:], op=ALU.mult, reads=[X.res, c.e_cum.res], writes=[c.Rt.res])
                yield
                dve.op("scalar_tensor_tensor", out=c.At[:], in0=kkn_, scalar=-1.0, in1=c.e_cump[:], op0=ALU.mult, op1=ALU.mult,
                       reads=[X.res, c.e_cump.res], writes=[c.At.res])
                dve.op("tensor_tensor", out=c.Bh[:], in0=bb_, in1=c.e_tc[:], op=ALU.mult, reads=[X.res, c.e_tc.res], writes=[c.Bh.res])
                dve.op("tensor_tensor", out=c.Kh[:], in0=k2_, in1=c.e_tc[:], op=ALU.mult, reads=[X.res, c.e_tc.res], writes=[c.Kh.res])
                dve.op("tensor_tensor", out=c.Dg[:], in0=c.e_tot[:], in1=I8[:], op=ALU.mult, reads=[c.e_tot.res, I8.res], writes=[c.Dg.res])
                for (src, dstT, eng) in ((c.Bt, c.BtT, dve), (c.Kt, c.KtT, act), (c.Rt, c.RtT, dve), (c.At, c.AtT, act)):
                    bk = nb()
                    for h in range(8):
                        pe.op("transpose", out=bk[0:64, hs(h)], in_=src[:, hs(h)], identity=I64, reads=[src.res, ident_f.res], writes=[bk.res], signal=(h == 7))
                    yield
                    if eng is act:
                        act.op("copy", out=dstT[:], in_=bk[0:64, :], reads=[bk.res], writes=[dstT.res])
                    else:
                        dve.op("tensor_copy", out=dstT[:], in_=bk[0:64, :], reads=[bk.res], writes=[dstT.res])
                yield
                P, PT = c.Pm0, c.PTm0
                for (lhs, rhs, dst_, msk) in ((c.AtT, c.BtT, P, SLm), (c.BtT, c.AtT, PT, SLTm), (c.KtT, c.AtT, c.MakT, SLTm), (c.BtT, c.RtT, c.NrbT, LETm), (c.KtT, c.RtT, c.NrkT, LETm)):
                    bk = nb()
                    mm_heads(bk, (lhs,), rhs)
                    yield
                    dve.op("tensor_tensor", out=dst_[:], in0=bk[0:64, :], in1=msk[:], op=ALU.mult, reads=[bk.res, msk.res], writes=[dst_.res])
                yield
                Qs = [c.Q0, c.Q1]
                Pms = [c.Pm0, c.Pm1]
                PTms = [c.PTm0, c.PTm1]
                Qc = Qs[0]
                dve.op("tensor_tensor", out=Qc[:], in0=PT[:], in1=I8[:], op=ALU.add, reads=[PT.res, I8.res], writes=[Qc.res])
                for lvl in range(5):
                    Pn, PTn = Pms[(lvl + 1) % 2], PTms[(lvl + 1) % 2]
                    bk = nb()
                    mm_heads(bk, (PT,), P)
                    if lvl < 4:
                        bk2 = nb()
                        mm_heads(bk2, (P,), PT)
                    yield
                    act.op("copy", out=Pn[:], in_=bk[0:64, :], reads=[bk.res], writes=[Pn.res])
                    if lvl < 4:
                        dve.op("tensor_copy", out=PTn[:], in_=bk2[0:64, :], reads=[bk2.res], writes=[PTn.res])
                    yield
                    Qn = Qs[(lvl + 1) % 2]
                    bk3 = nb()
                    mm_heads(bk3, (Pn,), Qc, extra=[(None, Qc)])
                    yield
                    act.op("copy", out=Qn[:], in_=bk3[0:64, :], reads=[bk3.res], writes=[Qn.res])
                    P, PT, Qc = Pn, PTn, Qn
                TT = Qc
                yield
                bk = nb()
                mm_heads(bk, (c.MakT,), VV)
                bka = nb()
                mm_heads(bka, (TT,), c.At)
                yield
                act.op("copy", out=c.X2[:], in_=bk[0:64, :], reads=[bk.res], writes=[c.X2.res])
                dve.op("tensor_copy", out=c.Ahat[:], in_=bka[0:64, :], reads=[bka.res], writes=[c.Ahat.res])
                yield
                bk = nb()
                mm_heads(bk, (TT,), c.X2)
                bkp = nb()
                mm_heads(bkp, (c.Ahat,), c.Bh, extra=[(None, c.Dg)])
                bkr = nb()
                mm_heads(bkr, (c.Ahat,), c.NrbT, extra=[(None, c.RtT)])
                yield
                dve.op("tensor_copy", out=c.What[:], in_=bk[0:64, :], reads=[bk.res], writes=[c.What.res])
                act.op("copy", out=c.PhiT[:], in_=bkp[0:64, :], reads=[bkp.res], writes=[c.PhiT.res])
                act.op("copy", out=c.RhT[:], in_=bkr[0:64, :], reads=[bkr.res], writes=[c.RhT.res])
                yield
                bko = nb()
                mm_heads(bko, (c.NrbT,), c.What, extra=[((c.NrkT,), VV), ((c.RhT,), ST)])
                bks = nb()
                mm_heads(bks, (c.Bh,), c.What, extra=[((c.Kh,), VV), ((c.PhiT,), ST)])
                act.op("copy", out=ST[:], in_=bks[0:64, :], reads=[bks.res], writes=[ST.res])
                yield
                yo = ya_o[ci % 2]
                Osb, cen, sqv, mean8, var8 = c.Osb, c.cen, c.sqv, c.mean8, c.var8
                dve.op("tensor_copy", out=Osb[:], in_=bko[0:64, :], reads=[bko.res], writes=[Osb.res])
                dve.op("tensor_reduce", out=mean8[:], in_=v3(Osb[:]), axis=AX.X, op=ALU.add, reads=[Osb.res], writes=[mean8.res])
                yield
                dve.op("tensor_scalar", out=mean8[:], in0=mean8[:], scalar1=1.0 / 64, scalar2=None, op0=ALU.mult, reads=[mean8.res], writes=[mean8.res])
                dve.op("tensor_tensor", out=v3(cen[:]), in0=v3(Osb[:]), in1=bc3(mean8[:], [64, 8, 64]), op=ALU.subtract,
                       reads=[Osb.res, mean8.res], writes=[cen.res])
                yield
                dve.op("tensor_tensor", out=sqv[:], in0=cen[:], in1=cen[:], op=ALU.mult, reads=[cen.res], writes=[sqv.res])
                dve.op("tensor_reduce", out=var8[:], in_=v3(sqv[:]), axis=AX.X, op=ALU.add, reads=[sqv.res], writes=[var8.res])
                yield
                act.op("activation", out=var8[:], in_=var8[:], func=AF.Sqrt, scale=1.0 / 64, bias=eps_ln[:, 0:1], reads=[var8.res, eps_ln.res], writes=[var8.res])
                yield
                dve.op("reciprocal", out=var8[:], in_=var8[:], reads=[var8.res], writes=[var8.res])
                dve.op("tensor_tensor", out=v3(cen[:]), in0=v3(cen[:]), in1=bc3(var8[:], [64, 8, 64]), op=ALU.mult, reads=[cen.res, var8.res], writes=[cen.res])
                yield
                dve.op("tensor_tensor", out=cen[:], in0=cen[:], in1=lnw_bc[:], op=ALU.mult, reads=[cen.res, lnw_bc.res], writes=[cen.res])
                dve.op("tensor_tensor", out=v3(sqv[:]), in0=v3(v_), in1=bc3(rkin[b_][:], [64, 8, 64]), op=ALU.mult, reads=[X.res, rkin[b_].res], writes=[sqv.res])
                yield
                dve.op("tensor_tensor", out=cen[:], in0=cen[:], in1=lnb_bc[:], op=ALU.add, reads=[cen.res, lnb_bc.res], writes=[cen.res])
                yield
                dve.op("tensor_tensor", out=cen[:], in0=cen[:], in1=sqv[:], op=ALU.add, reads=[cen.res, sqv.res], writes=[cen.res])
                yield
                dve.op("tensor_tensor", out=yo[:], in0=cen[:], in1=gin[b_][:], op=ALU.mult, reads=[cen.res, gin[b_].res], writes=[yo.res])
                sp.dma(s_ya[tsl, :], yo[:], d_co[ci % 2], reads=[yo.res])

            for ci in range(min(2, NCH)):
                load_c(ci)
            for pr in range(0, NCH, 2):
                for ci in (pr + 2, pr + 3):
                    if ci < NCH:
                        load_c(ci)
                gens = [chunk_gen(ci, ctxs[ci % 2]) for ci in (pr, pr + 1) if ci < NCH]
                if LOCKSTEP == 0:
                    for g_ in gens:
                        for _ in g_:
                            pass
                    gens = []
                while gens:
                    for g_ in list(gens):
                        try:
                            for _ in range(LOCKSTEP):
                                next(g_)
                        except StopIteration:
                            gens.remove(g_)
            kb.barrier()

        def make_conv(esc):
            cld = [sb(kb, esc, f"cld{i}", [128, D], F32) for i in range(3)]
            cst2 = [sb(kb, esc, f"cst2_{i}", [128, D], BF16) for i in range(2)]
            d_cl = [kb.dsem(f"d_cl{i}") for i in range(3)]
            d_cs = kb.dsem("d_cs")

            def gen():
                dv = s_uv.rearrange("(p j) d -> p j d", p=128)
                n = 0
                for ti_, src_t in enumerate((peer_u, peer_v)):
                    sv = src_t.rearrange("(p j) d -> p j d", p=128)
                    for j in range(128):
                        lb = cld[n % 3]
                        sp.dma(lb[:], sv[:, j, :], d_cl[n % 3], writes=[lb.res])
                        gr = cst2[n % 2]
                        dve.op("tensor_copy", out=gr[:], in_=lb[:], reads=[lb.res], writes=[gr.res])
                        sp.dma(dv[:, j, ti_ * D:(ti_ + 1) * D], gr[:], d_cs, reads=[gr.res])
                        n += 1
                        yield
            return gen()

        if "D" in phases:
          with ExitStack() as ed:
            NB = S // 256
            NG = S // 512
            KTa = sb(kb, ed, "KTa", [96, S], BF16)
            QTa = sb(kb, ed, "QTa", [96, S], BF16)
            Vall = sb(kb, ed, "Vall", [128, NT, 8 * 65], BF16)
            CB = sb(kb, ed, "CB", [128, 4, 512], BF16)
            negb = sb(kb, ed, "negb", [128, 32, 32], F32)
            ownb = sb(kb, ed, "ownb", [128, 32, 32], F32)
            kmf = sb(kb, ed, "kmf", [96, 32], F32)
            kmb = sb(kb, ed, "kmb", [96, 32], BF16)
            gm = sb(kb, ed, "gm", [128, NT, 32], F32)
            gq = sb(kb, ed, "gq", [128, NT, 32], F32)
            b2 = sb(kb, ed, "b2", [128, NT, 32], BF16)
            mx = sb(kb, ed, "mx", [128, NT], F32)
            NPT = 4
            PT = [sb(kb, ed, f"PT{i}", [128, 512], BF16) for i in range(NPT)]
            OT = [sb(kb, ed, f"OT{i}", [65, 512], F32) for i in range(2)]
            rden = sb(kb, ed, "rden", [128, 4, 1], F32)
            yb_o = [sb(kb, ed, f"yb_o{i}", [128, 4, 64], BF16) for i in range(2)]
            d_kq = kb.dsem("d_kq")
            d_yb = [kb.dsem(f"d_yb{i}") for i in range(2)]
            conv = make_conv(ed)

            sp.dma(KTa[64:96, :], emat[:, :], d_c, writes=[KTa.res])
            sp.dma(CB[:], cb_mask[:, :, :], d_c, writes=[CB.res])
            sp.dma(negb[:], cneg[0].partition_broadcast(128), d_c, writes=[negb.res])
            sp.dma(ownb[:], cneg[1].partition_broadcast(128), d_c, writes=[ownb.res])
            sp.dma(Vall[:], s_va.rearrange("(tt p) h e -> p tt (h e)", p=128), d_c, writes=[Vall.res])
            kb.fence(d_c)
            dve.op("memset", kmf[:], 0.0, writes=[kmf.res])
            dve.op("memset", kmb[:], 0.0, writes=[kmb.res])
            bOs = [banks[3], banks[5]]
            bS = [banks[0], banks[1], banks[2], banks[4]]
            bY = banks[6]
            it = 0
            for h in range(8):
                sp.dma(KTa[0:64, :], s_kt[h], d_kq, writes=[KTa.res])
                sp.dma(QTa[0:64, :], s_qt[h], d_kq, writes=[QTa.res])
                kb.fence(d_kq)
                dve.op("tensor_reduce", out=kmf[0:64, 0:NB], in_=KTa[0:64, :].rearrange("p (n k) -> p n k", k=256), axis=AX.X, op=ALU.add,
                       reads=[KTa.res], writes=[kmf.res])
                dve.op("tensor_scalar", out=kmb[0:64, 0:NB], in0=kmf[0:64, 0:NB], scalar1=1.0 / 256, scalar2=None, op0=ALU.mult,
                       reads=[kmf.res], writes=[kmb.res])
                TPB = 16
                nbk = (NT + TPB - 1) // TPB
                for bi in range(nbk):
                    bk = banks[bi]
                    t_lo, t_hi = bi * TPB, min(NT, (bi + 1) * TPB)
                    for tt in range(t_lo, t_hi):
                        pe.op("matmul", bk[:, (tt - t_lo) * 32:(tt - t_lo + 1) * 32], lhsT=QTa[0:64, tt * 128:(tt + 1) * 128], rhs=kmb[0:64, 0:32],
                              start=True, stop=True, reads=[QTa.res, kmb.res], writes=[bk.res], signal=(tt == t_hi - 1))
                    nq = (t_hi - t_lo) // 2
                    dve.op("tensor_tensor", out=gm[:, t_lo:t_hi, :].rearrange("p (q two) n -> p q two n", two=2),
                           in0=bk[:, 0:(t_hi - t_lo) * 32].rearrange("p (q two n) -> p q two n", two=2, n=32),
                           in1=negb[:, t_lo // 2:t_lo // 2 + nq, :].unsqueeze(2).to_broadcast([128, nq, 2, 32]), op=ALU.add,
                           reads=[bk.res, negb.res], writes=[gm.res])
                cur = gm
                for rnd in range(2):
                    dve.op("tensor_reduce", out=mx[:], in_=cur[:], axis=AX.X, op=ALU.max, reads=[cur.res], writes=[mx.res])
                    dve.op("tensor_tensor", out=b2[:], in0=cur[:], in1=bc3(mx[:], [128, NT, 32]), op=ALU.is_equal, reads=[cur.res, mx.res], writes=[b2.res])
                    dve.op("scalar_tensor_tensor", out=gq[:], in0=b2[:], scalar=NEG, in1=cur[:], op0=ALU.mult, op1=ALU.add,
                           reads=[b2.res, cur.res], writes=[gq.res])
                    cur = gq
                dve.op("tensor_reduce", out=mx[:], in_=gq[:], axis=AX.X, op=ALU.max, reads=[gq.res], writes=[mx.res])
                dve.op("tensor_scalar", out=mx[:], in0=mx[:], scalar1=-1e29, scalar2=None, op0=ALU.max, reads=[mx.res], writes=[mx.res])
                dve.op("tensor_tensor", out=gq[:], in0=gm[:], in1=bc3(mx[:], [128, NT, 32]), op=ALU.is_ge, reads=[gm.res, mx.res], writes=[gq.res])
                dve.op("tensor_scalar", out=gq[:], in0=gq[:], scalar1=1.0, scalar2=1e30, op0=ALU.subtract, op1=ALU.mult, reads=[gq.res], writes=[gq.res])
                dve.op("tensor_tensor", out=b2[:].rearrange("p (q two) n -> p q two n", two=2), in0=gq[:].rearrange("p (q two) n -> p q two n", two=2),
                       in1=ownb[:, 0:NT // 2, :].unsqueeze(2).to_broadcast([128, NT // 2, 2, 32]), op=ALU.add, reads=[gq.res, ownb.res], writes=[b2.res])
                for t4 in range(0, NT, 4):
                    bk = banks[4 + (t4 // 4) % 2]
                    for j in range(4):
                        pe.op("matmul", bk[64:96, j * 128:(j + 1) * 128], lhsT=b2[:, t4 + j, :], rhs=ident_b[:, :], start=True, stop=True,
                              reads=[b2.res, ident_b.res], writes=[bk.res], signal=(j == 3))
                    act.op("copy", out=QTa[64:96, t4 * 128:(t4 + 4) * 128], in_=bk[64:96, :], reads=[bk.res], writes=[QTa.res])
                its = [(g, kt) for g in range(NG) for kt in range(4 * g + 4)]
                n_it = len(its)
                NBS = len(bS)

                def emit_qk(i):
                    g, kt = its[i]
                    bs = bS[i % NBS]
                    ksl = slice(kt * 128, (kt + 1) * 128)
                    diag = kt >= 4 * g
                    pe.op("matmul", bs[:, :], lhsT=KTa[0:96, ksl], rhs=QTa[0:96, g * 512:(g + 1) * 512], start=True, stop=not diag,
                          reads=[KTa.res, QTa.res], writes=[bs.res], signal=not diag)
                    if diag:
                        pe.op("matmul", bs[:, :], lhsT=ident_b[:, :], rhs=CB[:, kt - 4 * g, :], start=False, stop=True,
                              reads=[ident_b.res, CB.res], writes=[bs.res])

                def emit_exp(i):
                    act.op("activation", out=PT[i % NPT][:], in_=bS[i % NBS][:, :], func=AF.Exp, reads=[bS[i % NBS].res], writes=[PT[i % NPT].res])

                def emit_pv(i):
                    g, kt = its[i]
                    nkt = 4 * g + 4
                    bo = bOs[g % 2]
                    pe.op("matmul", bo[0:65, :], lhsT=Vall[:, kt, h * 65:(h + 1) * 65], rhs=PT[i % NPT][:], start=(kt == 0), stop=(kt == nkt - 1),
                          reads=[Vall.res, PT[i % NPT].res], writes=[bo.res], signal=(kt == nkt - 1))

                def epi_a(g):
                    dve.op("tensor_copy", out=OT[g % 2][:], in_=bOs[g % 2][0:65, :], reads=[bOs[g % 2].res], writes=[OT[g % 2].res])

                def epi_b(g):
                    ot = OT[g % 2]
                    for j in range(4):
                        pe.op("transpose", out=bY[:, j * 65:(j + 1) * 65], in_=ot[:, j * 128:(j + 1) * 128], identity=ident_f[0:65, 0:65],
                              reads=[ot.res, ident_f.res], writes=[bY.res], signal=(j == 3))
                    yo = yb_o[g % 2]
                    yv = bY[:, 0:260].rearrange("p (j e) -> p j e", j=4)
                    dve.op("reciprocal", out=rden[:], in_=yv[:, :, 64:65], reads=[bY.res], writes=[rden.res])
                    dve.op("tensor_tensor", out=yo[:], in0=yv[:, :, 0:64], in1=rden[:].to_broadcast([128, 4, 64]), op=ALU.mult,
                           reads=[bY.res, rden.res], writes=[yo.res])
                    sp.dma(s_yb[g * 512:(g + 1) * 512, h * 64:(h + 1) * 64].rearrange("(j p) d -> p j d", p=128), yo[:], d_yb[g % 2], reads=[yo.res])

                LA = 2
                pend = []
                for i in range(min(LA, n_it)):
                    emit_qk(i)
                for i in range(n_it):
                    emit_exp(i)
                    if i + LA < n_it:
                        emit_qk(i + LA)
                    emit_pv(i)
                    g, kt = its[i]
                    pend = [(c - 1, gg) for (c, gg) in pend]
                    for (c, gg) in pend:
                        if c == 0:
                            epi_b(gg)
                    pend = [(c, gg) for (c, gg) in pend if c > 0]
                    if kt == 4 * g + 3:
                        epi_a(g)
                        pend.append((3, g))
                        for _cv in range(2):
                            next(conv, None)
                for (c, gg) in pend:
                    epi_b(gg)
            for _ in conv:
                pass
            kb.barrier()


        if "E" in phases:
          with ExitStack() as ee:
            NGR = 11
            GK = 2
            Wb = sb(kb, ee, "Wb", [128, 8, D], BF16)
            Wo = sb(kb, ee, "Wo", [128, 8, D], BF16)
            Wq = sb(kb, ee, "Wq", [128, 8, 2048], BF16)
            skT = sb(kb, ee, "skT", [128, 16, 128], F32)
            bc_gate1 = sb(kb, ee, "bc_gate1", [128, D], F32)
            bc_gs2 = sb(kb, ee, "bc_gs2", [128, D], F32)
            bc_shift2 = sb(kb, ee, "bc_shift2", [128, D], F32)
            bc_gate2 = sb(kb, ee, "bc_gate2", [128, D], F32)
            d_gb = [kb.dsem(f"d_gb{i}") for i in range(NGR)]
            d_gg = [kb.dsem(f"d_gg{i}") for i in range(NGR)]
            grow = [sb(kb, ee, f"grow{i}", [128, 2 * D], BF16) for i in range(NGR)]
            dg = [sb(kb, ee, f"dg{i}", [128, 128], BF16) for i in range(4)]
            tok_t = sb(kb, ee, "tok_t", [128, NTE], I32)
            d_eg = [kb.dsem(f"d_eg{i}") for i in range(2)]
            xt = [sb(kb, ee, f"ext{i}", [128, D], F32) for i in range(1)]
            ya_t = [sb(kb, ee, f"ya_t{i}", [128, RW], BF16) for i in range(1)]
            yb_t = [sb(kb, ee, f"yb_t{i}", [128, RW], BF16) for i in range(1)]
            gt_t = [sb(kb, ee, f"gt_t{i}", [128, 2048], BF16) for i in range(1)]
            d_ei = [kb.dsem(f"d_ei{i}") for i in range(2)]
            d_eo = kb.dsem("d_eo")
            yT = sb(kb, ee, "yT", [128, 8, 128], BF16)
            m0 = sb(kb, ee, "m0", [128, D], F32)
            msum = sb(kb, ee, "msum", [128, D], BF16)
            msT = sb(kb, ee, "msT", [128, 8, 128], BF16)
            x1 = sb(kb, ee, "x1", [128, D], F32)
            h2 = sb(kb, ee, "h2", [128, D], F32)
            h2b = sb(kb, ee, "h2b", [128, D], BF16)
            h2T = sb(kb, ee, "h2T", [128, 8, 128], BF16)
            ejunk = sb(kb, ee, "ejunk", [128, D], F32)
            djunk = sb(kb, ee, "djunk", [128, D], BF16)
            qT = sb(kb, ee, "qT", [128, 16, 128], F32)
            s12 = sb(kb, ee, "s12", [128, 16, 128], F32)
            work = sb(kb, ee, "work", [128, 256], F32)
            topv = sb(kb, ee, "topv", [128, 16, 16], F32)
            topi = sb(kb, ee, "topi", [128, 16, 16], U32)
            topif = sb(kb, ee, "topif", [128, 16, 16], F32)
            sc = sb(kb, ee, "sc", [128, 8, 16], F32)
            ci = sb(kb, ee, "ci", [128, 8, 16], U32)
            cdv = sb(kb, ee, "cdv", [128, 8, 16], U32)
            cdf = sb(kb, ee, "cdf", [128, 8, 16], F32)
            cmf = sb(kb, ee, "cmf", [128, 8, 16], F32)
            e1 = sb(kb, ee, "e1", [128, 8, 16], F32)
            e2 = sb(kb, ee, "e2", [128, 8, 16], F32)
            idxf = sb(kb, ee, "idxf", [128, 128], F32)
            idx = sb(kb, ee, "idx", [128, 128], I32)
            iota16 = sb(kb, ee, "iota16", [128, 16], F32)
            ssum = sb(kb, ee, "ssum", [128, 8], F32)
            gw = sb(kb, ee, "gw", [128, 8, 16], F32)
            hu = sb(kb, ee, "hu", [128, 128], F32)
            actv = sb(kb, ee, "actv", [128, 128], F32)
            acc = m0
            ss2 = sb(kb, ee, "ss2", [128, 1], F32)
            rstd2 = sb(kb, ee, "rstd2", [128, 1], F32)
            bT = banks[0][:].bitcast(BF16)
            bankT = banks[0]

            ld_i = [0]
            class _G:
                def __init__(self, t_):
                    self.t = t_
                    self.res = t_.res

                def __getitem__(self, k):
                    return self.t.t[:].bitcast(F32)[k]

            gbuf = [_G(g_) for g_ in grow]
            NGB = NGR
            def stage_load(src_ap):
                i = ld_i[0] % NGB
                ld_i[0] += 1
                sp.dma(gbuf[i][:, :], src_ap, d_gb[i], writes=[gbuf[i].res])
                return gbuf[i]

            wb_v = w_branch.rearrange("b (k p) n -> p b k n", p=128)
            for bk_ in range(8):
                g_ = stage_load(wb_v[:, bk_ // 4, bk_ % 4, :])
                dve.op("tensor_copy", out=Wb[:, bk_, :], in_=g_[:, :], reads=[g_.res], writes=[Wb.res])
            wo_v = w_out.rearrange("(k p) n -> p k n", p=128)
            for k in range(8):
                g_ = stage_load(wo_v[:, k, :])
                act.op("copy", out=Wo[:, k, :], in_=g_[:, :], reads=[g_.res], writes=[Wo.res])
            wq_v = w_peer_q.rearrange("(k p) n -> p k n", p=128)
            for k in range(8):
                for hf in range(2):
                    g_ = stage_load(wq_v[:, k, hf * 1024:(hf + 1) * 1024])
                    if hf == 0:
                        dve.op("tensor_copy", out=Wq[:, k, 0:1024], in_=g_[:, :], reads=[g_.res], writes=[Wq.res])
                    else:
                        act.op("copy", out=Wq[:, k, 1024:2048], in_=g_[:, :], reads=[g_.res], writes=[Wq.res])
            for cc in range(16):
                hp, side = cc // 2, cc % 2
                i = ld_i[0] % NGB
                ld_i[0] += 1
                sp.dma(gbuf[i][:, 0:128], peer_sub_keys[side, hp], d_gb[i], writes=[gbuf[i].res])
                bk = banks[1 + cc % 2]
                pe.op("transpose", out=bk[:, 0:128], in_=gbuf[i][:, 0:128], identity=ident_f[:], reads=[gbuf[i].res, ident_f.res], writes=[bk.res])
                dve.op("tensor_copy", out=skT[:, cc, :], in_=bk[:, 0:128], reads=[bk.res], writes=[skT.res])
            sp.dma(bc_gate1[:], s_ada[0, 2 * D:3 * D].partition_broadcast(128), d_c, writes=[bc_gate1.res])
            sp.dma(bc_gs2[:], s_ada[0, 4 * D:5 * D].partition_broadcast(128), d_c, writes=[bc_gs2.res])
            sp.dma(bc_shift2[:], s_ada[0, 3 * D:4 * D].partition_broadcast(128), d_c, writes=[bc_shift2.res])
            sp.dma(bc_gate2[:], s_ada[0, 5 * D:6 * D].partition_broadcast(128), d_c, writes=[bc_gate2.res])
            sp.dma(ejunk[:], norm2_g.partition_broadcast(128), d_c, writes=[ejunk.res])
            sp.dma(tok_t[:], tokidx[:, :], d_c, writes=[tok_t.res])
            kb.fence(d_c)
            kb.fence(d_eo)
            dve.op("scalar_tensor_tensor", out=bc_gs2[:], in0=bc_gs2[:], scalar=1.0, in1=ejunk[:], op0=ALU.add, op1=ALU.mult,
                   reads=[bc_gs2.res, ejunk.res], writes=[bc_gs2.res])
            pool.op("iota", iota16[:], pattern=[[1, 16]], base=0, channel_multiplier=0, allow_small_or_imprecise_dtypes=True, writes=[iota16.res])

            def gather(dst_tile, table_ap, idx_ap, dsem_, use_idx=False):
                deps = [tok_t.res] + ([idx.res] if use_idx else [])
                pool._pre(deps, [dst_tile.res])
                ins = nc.gpsimd.indirect_dma_start(out=dst_tile[:, :], out_offset=None, in_=table_ap,
                                                   in_offset=bass.IndirectOffsetOnAxis(ap=idx_ap, axis=0))
                dsem_.count += 16
                ins.then_inc(dsem_.sem, 16)
                pool._post((dsem_.sem, dsem_.count, dsem_), deps, [dst_tile.res])

            def load_e(tt):
                ia = tok_t[:, tt:tt + 1]
                gather(xt[0], x[:, :], ia, d_eg[0])
                gather(ya_t[0], s_ya[:, :], ia, d_eg[0])
                gather(yb_t[0], s_yb[:, :], ia, d_eg[0])
                gather(gt_t[0], s_gates[:, :], ia, d_eg[0])

            def v4(t_, hp_dim=8):
                return t_[:].rearrange("p h (i j) -> p h i j", i=16)

            gi = [0]
            load_e(0)
            for tt in range(NTE):
                tsl = slice(tt * 128, (tt + 1) * 128)
                xc, yat, ybt, gtt = xt[0], ya_t[0], yb_t[0], gt_t[0]
                for j in range(4):
                    pe.op("transpose", out=bT[:, j * 128:(j + 1) * 128], in_=yat[:, j * 128:(j + 1) * 128], identity=ident_b[:],
                          reads=[yat.res, ident_b.res], writes=[bankT.res], signal=False)
                for j in range(4):
                    pe.op("transpose", out=bT[:, (4 + j) * 128:(5 + j) * 128], in_=ybt[:, j * 128:(j + 1) * 128], identity=ident_b[:],
                          reads=[ybt.res, ident_b.res], writes=[bankT.res], signal=(j == 3))
                dve.op("tensor_copy", out=yT[:], in_=bT[:, :].rearrange("p (k t) -> p k t", k=8), reads=[bankT.res], writes=[yT.res])
                for br in range(2):
                    for hf in range(2):
                        bk = banks[1 + br * 2 + hf]
                        for k in range(4):
                            pe.op("matmul", bk[:, :], lhsT=yT[:, br * 4 + k, :], rhs=Wb[:, br * 4 + k, hf * 512:(hf + 1) * 512], start=(k == 0), stop=(k == 3),
                                  reads=[yT.res, Wb.res], writes=[bk.res], signal=(k == 3))
                for hf in range(2):
                    dve.op("tensor_tensor", out=m0[:, hf * 512:(hf + 1) * 512], in0=banks[1 + hf][:, :], in1=gtt[:, hf * 512:(hf + 1) * 512], op=ALU.mult,
                           reads=[banks[1 + hf].res, gtt.res], writes=[m0.res])
                for hf in range(2):
                    dve.op("tensor_tensor", out=ejunk[:, hf * 512:(hf + 1) * 512], in0=banks[3 + hf][:, :], in1=gtt[:, 1024 + hf * 512:1024 + (hf + 1) * 512],
                           op=ALU.mult, reads=[banks[3 + hf].res, gtt.res], writes=[ejunk.res])
                dve.op("tensor_tensor", out=msum[:], in0=m0[:], in1=ejunk[:], op=ALU.add, reads=[m0.res, ejunk.res], writes=[msum.res])
                for k in range(8):
                    pe.op("transpose", out=bT[:, k * 128:(k + 1) * 128], in_=msum[:, k * 128:(k + 1) * 128], identity=ident_b[:],
                          reads=[msum.res, ident_b.res], writes=[bankT.res], signal=(k == 7))
                act.op("copy", out=msT[:], in_=bT[:, :].rearrange("p (k t) -> p k t", k=8), reads=[bankT.res], writes=[msT.res])
                for hf in range(2):
                    bk = banks[5 + hf]
                    for k in range(8):
                        pe.op("matmul", bk[:, :], lhsT=msT[:, k, :], rhs=Wo[:, k, hf * 512:(hf + 1) * 512], start=(k == 0), stop=(k == 7),
                              reads=[msT.res, Wo.res], writes=[bk.res], signal=(k == 7))
                    dve.op("tensor_tensor", out=x1[:, hf * 512:(hf + 1) * 512], in0=bk[:, :], in1=bc_gate1[:, hf * 512:(hf + 1) * 512], op=ALU.mult,
                           reads=[bk.res, bc_gate1.res], writes=[x1.res])
                dve.op("tensor_tensor", out=x1[:], in0=x1[:], in1=xc[:], op=ALU.add, reads=[x1.res, xc.res], writes=[x1.res])
                if tt + 1 < NTE:
                    load_e(tt + 1)
                act.op("activation", out=ejunk[:], in_=x1[:], func=AF.Square, accum_out=ss2[:], reads=[x1.res], writes=[ejunk.res, ss2.res])
                act.op("activation", out=rstd2[:], in_=ss2[:], func=AF.Sqrt, scale=1.0 / D, bias=eps_t[:, 0:1], reads=[ss2.res, eps_t.res], writes=[rstd2.res])
                dve.op("reciprocal", out=rstd2[:], in_=rstd2[:], reads=[rstd2.res], writes=[rstd2.res])
                dve.op("scalar_tensor_tensor", out=h2[:], in0=x1[:], scalar=rstd2[:, 0:1], in1=bc_gs2[:], op0=ALU.mult, op1=ALU.mult,
                       reads=[x1.res, rstd2.res, bc_gs2.res], writes=[h2.res])
                dve.op("tensor_tensor", out=h2[:], in0=h2[:], in1=bc_shift2[:], op=ALU.add, reads=[h2.res, bc_shift2.res], writes=[h2.res])
                act.op("copy", out=h2b[:], in_=h2[:], reads=[h2.res], writes=[h2b.res])
                for k in range(8):
                    pe.op("transpose", out=bT[:, k * 128:(k + 1) * 128], in_=h2b[:, k * 128:(k + 1) * 128], identity=ident_b[:],
                          reads=[h2b.res, ident_b.res], writes=[bankT.res], signal=(k == 7))
                dve.op("tensor_copy", out=h2T[:], in_=bT[:, :].rearrange("p (k t) -> p k t", k=8), reads=[bankT.res], writes=[h2T.res])
                for qg in range(4):
                    bk = banks[1 + qg]
                    for c4 in range(4):
                        cc = qg * 4 + c4
                        for k in range(8):
                            pe.op("matmul", bk[:, c4 * 128:(c4 + 1) * 128], lhsT=Wq[:, k, cc * 128:(cc + 1) * 128], rhs=h2T[:, k, :], start=(k == 0), stop=(k == 7),
                                  reads=[Wq.res, h2T.res], writes=[bk.res], signal=(k == 7 and c4 == 3))
                    if qg % 2 == 0:
                        act.op("copy", out=qT[:, qg * 4:(qg + 1) * 4, :], in_=bk[:, :].rearrange("p (c t) -> p c t", c=4), reads=[bk.res], writes=[qT.res])
                    else:
                        dve.op("tensor_copy", out=qT[:, qg * 4:(qg + 1) * 4, :], in_=bk[:, :].rearrange("p (c t) -> p c t", c=4), reads=[bk.res], writes=[qT.res])
                for qg in range(4):
                    bk = banks[5 + qg % 3]
                    for c4 in range(4):
                        cc = qg * 4 + c4
                        pe.op("matmul", bk[:, c4 * 128:(c4 + 1) * 128], lhsT=qT[:, cc, :], rhs=skT[:, cc, :], start=True, stop=True,
                              reads=[qT.res, skT.res], writes=[bk.res], signal=(c4 == 3))
                    if qg % 2 == 0:
                        act.op("copy", out=s12[:, qg * 4:(qg + 1) * 4, :], in_=bk[:, :].rearrange("p (c t) -> p c t", c=4), reads=[bk.res], writes=[s12.res])
                    else:
                        dve.op("tensor_copy", out=s12[:, qg * 4:(qg + 1) * 4, :], in_=bk[:, :].rearrange("p (c t) -> p c t", c=4), reads=[bk.res], writes=[s12.res])
                for cc in range(16):
                    dve.op("max", out=topv[:, cc, 0:8], in_=s12[:, cc, :], reads=[s12.res], writes=[topv.res])
                    dve.op("max_index", out=topi[:, cc, 0:8], in_max=topv[:, cc, 0:8], in_values=s12[:, cc, :], reads=[topv.res, s12.res], writes=[topi.res])
                    dve.op("match_replace", out=work[:, 0:128], in_to_replace=topv[:, cc, 0:8], in_values=s12[:, cc, :], imm_value=NEG,
                           reads=[topv.res, s12.res], writes=[work.res])
                    dve.op("max", out=topv[:, cc, 8:16], in_=work[:, 0:128], reads=[work.res], writes=[topv.res])
                    dve.op("max_index", out=topi[:, cc, 8:16], in_max=topv[:, cc, 8:16], in_values=work[:, 0:128], reads=[topv.res, work.res], writes=[topi.res])
                dve.op("tensor_copy", out=topif[:], in_=topi[:], reads=[topi.res], writes=[topif.res])
                tv = topv[:].rearrange("p (h s) i -> p h s i", s=2)
                tif = topif[:].rearrange("p (h s) i -> p h s i", s=2)
                dve.op("tensor_tensor", out=s12[:].rearrange("p (h a) (b j) -> p h (a b) j", a=2, j=16), in0=tv[:, :, 0, :].unsqueeze(3).to_broadcast([128, 8, 16, 16]),
                       in1=tv[:, :, 1, :].unsqueeze(2).to_broadcast([128, 8, 16, 16]), op=ALU.add, reads=[topv.res], writes=[s12.res])
                for hp in range(8):
                    dve.op("max", out=sc[:, hp, 0:8], in_=s12[:, 2 * hp:2 * hp + 2, :].rearrange("p a x -> p (a x)"), reads=[s12.res], writes=[sc.res])
                    dve.op("max_index", out=ci[:, hp, 0:8], in_max=sc[:, hp, 0:8], in_values=s12[:, 2 * hp:2 * hp + 2, :].rearrange("p a x -> p (a x)"), reads=[sc.res, s12.res], writes=[ci.res])
                    dve.op("match_replace", out=work[:], in_to_replace=sc[:, hp, 0:8], in_values=s12[:, 2 * hp:2 * hp + 2, :].rearrange("p a x -> p (a x)"), imm_value=NEG,
                           reads=[sc.res, s12.res], writes=[work.res])
                    dve.op("max", out=sc[:, hp, 8:16], in_=work[:], reads=[work.res], writes=[sc.res])
                    dve.op("max_index", out=ci[:, hp, 8:16], in_max=sc[:, hp, 8:16], in_values=work[:], reads=[sc.res, work.res], writes=[ci.res])
                dve.op("tensor_single_scalar", out=cdv[:], in_=ci[:], scalar=4, op=ALU.logical_shift_right, reads=[ci.res], writes=[cdv.res])
                dve.op("tensor_copy", out=cdf[:], in_=cdv[:], reads=[cdv.res], writes=[cdf.res])
                dve.op("tensor_single_scalar", out=cdv[:], in_=ci[:], scalar=15, op=ALU.bitwise_and, reads=[ci.res], writes=[cdv.res])
                dve.op("tensor_copy", out=cmf[:], in_=cdv[:], reads=[cdv.res], writes=[cmf.res])
                io_b = iota16[:].unsqueeze(1).unsqueeze(1).to_broadcast([128, 8, 16, 16])
                for (cf, side, edst) in ((cdf, 0, e1), (cmf, 1, e2)):
                    dve.op("tensor_tensor", out=qT[:].rearrange("p (h a) (b j) -> p h (a b) j", a=2, j=16), in0=cf[:].unsqueeze(3).to_broadcast([128, 8, 16, 16]), in1=io_b, op=ALU.is_equal,
                           reads=[cf.res, iota16.res], writes=[qT.res])
                    dve.op("tensor_tensor", out=qT[:].rearrange("p (h a) (b j) -> p h (a b) j", a=2, j=16), in0=qT[:].rearrange("p (h a) (b j) -> p h (a b) j", a=2, j=16), in1=tif[:, :, side, :].unsqueeze(2).to_broadcast([128, 8, 16, 16]), op=ALU.mult,
                           reads=[qT.res, topif.res], writes=[qT.res])
                    dve.op("tensor_reduce", out=edst[:], in_=qT[:].rearrange("p (h a) (b j) -> p h (a b) j", a=2, j=16), axis=AX.X, op=ALU.add, reads=[qT.res], writes=[edst.res])
                dve.op("scalar_tensor_tensor", out=idxf[:], in0=e1[:].rearrange("p h j -> p (h j)"), scalar=128.0, in1=e2[:].rearrange("p h j -> p (h j)"),
                       op0=ALU.mult, op1=ALU.add, reads=[e1.res, e2.res], writes=[idxf.res])
                dve.op("tensor_copy", out=idx[:], in_=idxf[:], reads=[idxf.res], writes=[idx.res])
                dve.op("tensor_tensor", out=gw[:], in0=sc[:], in1=sc[:, :, 0:1].to_broadcast([128, 8, 16]), op=ALU.subtract, reads=[sc.res], writes=[gw.res])
                act.op("activation", out=gw[:], in_=gw[:], func=AF.Exp, reads=[gw.res], writes=[gw.res])
                dve.op("tensor_reduce", out=ssum[:], in_=gw[:], axis=AX.X, op=ALU.add, reads=[gw.res], writes=[ssum.res])
                dve.op("reciprocal", out=ssum[:], in_=ssum[:], reads=[ssum.res], writes=[ssum.res])
                dve.op("tensor_tensor", out=gw[:], in0=gw[:], in1=bc3(ssum[:], [128, 8, 16]), op=ALU.mult, reads=[gw.res, ssum.res], writes=[gw.res])
                gwf = gw[:].rearrange("p h j -> p (h j)")
                for g8 in range(128 // GK):
                    ks = range(g8 * GK, (g8 + 1) * GK)
                    bufs = []
                    for k in ks:
                        i = gi[0] % NGR
                        gi[0] += 1
                        bufs.append(grow[i])
                        gather(grow[i], s_uv[:, :], idx[:, k:k + 1], d_gg[i], True)
                        dve.op("scalar_tensor_tensor", out=djunk[:], in0=grow[i][:, 0:D], scalar=1.0, in1=h2[:], op0=ALU.mult, op1=ALU.mult, accum_out=hu[:, k:k + 1],
                               reads=[grow[i].res, h2.res], writes=[djunk.res, hu.res])
                    gsl = slice(g8 * GK, (g8 + 1) * GK)
                    act.op("activation", out=actv[:, gsl], in_=hu[:, gsl], func=AF.Gelu, reads=[hu.res], writes=[actv.res])
                    dve.op("tensor_tensor", out=actv[:, gsl], in0=actv[:, gsl], in1=gwf[:, gsl], op=ALU.mult, reads=[actv.res, gw.res], writes=[actv.res])
                    for k, gb_ in zip(ks, bufs):
                        dgk = dg[k % 4]
                        act.op("activation", out=dgk[:], in_=ident_b[:], func=AF.Copy, scale=actv[:, k:k + 1], reads=[ident_b.res, actv.res], writes=[dgk.res])
                        for hf in range(2):
                            pe.op("matmul", banks[6 + hf][:, :], lhsT=dgk[:, :], rhs=gb_[:, D + hf * 512:D + (hf + 1) * 512], start=(k == 0), stop=(k == 127),
                                  reads=[dgk.res, gb_.res], writes=[banks[6 + hf].res], signal=(hf == 1))
                for hf in range(2):
                    dve.op("tensor_tensor", out=acc[:, hf * 512:(hf + 1) * 512], in0=banks[6 + hf][:, :], in1=bc_gate2[:, hf * 512:(hf + 1) * 512], op=ALU.mult,
                           reads=[banks[6 + hf].res, bc_gate2.res], writes=[acc.res])
                dve.op("tensor_tensor", out=acc[:], in0=acc[:], in1=x1[:], op=ALU.add, reads=[acc.res, x1.res], writes=[acc.res])
                sp.dma(out[tsl, :], acc[:], d_eo, reads=[acc.res])
                if dbg:
                    sp.dma(dbg_x1[tsl, :], x1[:], d_eo, reads=[x1.res])
                    sp.dma(dbg_idx[tsl, :], idx[:], d_eo, reads=[idx.res])
                    sp.dma(dbg_gw[tsl, :], gw[:].rearrange("p h j -> p (h j)"), d_eo, reads=[gw.res])
            kb.barrier()

        if dbg:
            dbg_a = nc.dram_tensor("dbg_a", [128, 48], F32, kind="ExternalOutput").ap()
            d_dbg = kb.dsem("d_dbg")
            sp.dma(dbg_a[:, 0:48], ada_col[:], d_dbg, reads=[ada_col.res])

        kb.barrier()
    return nc


def make_cmask():
    m = np.zeros((128, 2048), np.float32)
    m[:, 0:128] = np.eye(128, dtype=np.float32)
    let = np.triu(np.ones((64, 64), np.float32))
    slt = np.triu(np.ones((64, 64), np.float32), 1)
    m[0:64, 128:192] = let
    m[0:64, 512:1024] = np.tile(slt.T, (1, 8))
    m[0:64, 1024:1536] = np.tile(slt, (1, 8))
    m[0:64, 1536:2048] = np.tile(let, (1, 8))
    return m


def make_consts(S):
    import ml_dtypes
    cneg = np.zeros((2, 32, 32), np.float32)
    for qb in range(32):
        cneg[0, qb, qb:] = NEG
        cneg[1, qb, qb] = -NEG
    i = np.arange(128)[:, None]
    j = np.arange(512)[None, :]
    cb = np.stack([np.where(128 * dlt + i <= j, 0.0, NEG) for dlt in range(4)], axis=1).astype(np.float32)
    e = (np.arange(S)[None, :] // 256 == np.arange(32)[:, None]).astype(np.float32)
    return {"cneg": cneg.reshape(2, 1024), "cb_mask": cb.astype(ml_dtypes.bfloat16), "emat": e.astype(ml_dtypes.bfloat16)}


INPUT_NAMES = ["w_ada", "b_ada", "norm1_g", "w_in", "mu_rwkv", "w0", "w2_decay", "a0", "a2_iclr", "g2_gate", "k_k", "k_a",
               "r_k", "lnx_w", "lnx_b", "q_norm_g", "k_norm_g", "w_branch", "w_out", "norm2_g", "w_peer_q",
               "peer_sub_keys", "peer_u", "peer_v"]


def make_in_maps(inputs, cores, S=S_FULL, halves=2):
    shared = {k: np.ascontiguousarray(np.asarray(inputs[k], dtype=np.float32)) for k in INPUT_NAMES}
    shared["r_k"] = shared["r_k"].reshape(RW)
    shared["cmask"] = make_cmask()
    shared.update(make_consts(S))
    nte = S // 128 // halves
    maps = []
    for (b, hf) in cores:
        m = dict(shared)
        m["x"] = np.ascontiguousarray(np.asarray(inputs["x"][b, :S], dtype=np.float32))
        m["c"] = np.ascontiguousarray(np.asarray(inputs["c"][b], dtype=np.float32))
        m["tokidx"] = (hf * (S // halves) + np.arange(nte)[None, :] * 128 + np.arange(128)[:, None]).astype(np.int32)
        maps.append(m)
    return maps


def kernel(**inputs):
    nc = bass.Bass("TRN2", target_bir_lowering=False)
    build(nc)
    cores = [(b, hf) for b in range(4) for hf in range(2)]
    maps = make_in_maps(inputs, cores)
    res = run_bass_kernel_spmd(nc, maps, core_ids=list(range(8)))
    outs = [np.asarray(r["out"]) for r in res.results]
    return np.stack([np.concatenate([outs[2 * b], outs[2 * b + 1]], axis=0) for b in range(4)], axis=0)
```

```python
import numpy as np
from contextlib import ExitStack
import concourse.bass as bass
import concourse.mybir as mybir
from concourse.bass_utils import run_bass_kernel_spmd

F32 = mybir.dt.float32
BF16 = mybir.dt.bfloat16
U32 = mybir.dt.uint32
I32 = mybir.dt.int32
AF = mybir.ActivationFunctionType
ALU = mybir.AluOpType
AX = mybir.AxisListType

D = 1024
S_FULL = 8192
RW = 512
RWKV_COLS = 1824
IN_COLS = 5408
NEG = -1.0e30
LOCKSTEP = 0


class Res:
    __slots__ = ("name", "w", "r")

    def __init__(self, name):
        self.name = name
        self.w = None
        self.r = {}


class DSem:
    def __init__(self, sem):
        self.sem = sem
        self.count = 0


class Eng:
    def __init__(self, name, h, is_pe=False):
        self.name = name
        self.h = h
        self.is_pe = is_pe
        self.sem = None
        self.count = 0
        self.waited = {}

    def wait(self, dep):
        if dep is None:
            return
        sem, val, src = dep
        if src is self and self.is_pe:
            return
        if isinstance(src, DSem):
            val = max(val, src.count)
        key = sem.num
        if self.waited.get(key, 0) >= val:
            return
        self.h.wait_ge(sem, val)
        self.waited[key] = val

    def _pre(self, reads, writes):
        for r in reads:
            self.wait(r.w)
        for w in writes:
            self.wait(w.w)
            for d in w.r.values():
                self.wait(d)

    def _post(self, dep, reads, writes):
        for r in reads:
            old = r.r.get(dep[0].num)
            if old is None or old[1] < dep[1]:
                r.r[dep[0].num] = dep
        for w in writes:
            w.w = dep
            w.r = {}

    def op(self, fn, *args, reads=(), writes=(), signal=True, **kw):
        self._pre(reads, writes)
        ins = getattr(self.h, fn)(*args, **kw)
        if signal:
            self.count += 1
            ins.then_inc(self.sem, 1)
            val = self.count
        else:
            val = self.count + 1
        self._post((self.sem, val, self), reads, writes)
        return ins

    def dma(self, out, in_, dsem, reads=(), writes=(), **kw):
        self._pre(reads, writes)
        ins = self.h.dma_start(out=out, in_=in_, **kw)
        dsem.count += 16
        ins.then_inc(dsem.sem, 16)
        self._post((dsem.sem, dsem.count, dsem), reads, writes)
        return ins


class KB:
    def __init__(self, nc, es):
        self.nc = nc
        self.es = es
        self.pe = Eng("pe", nc.tensor, True)
        self.act = Eng("act", nc.scalar)
        self.dve = Eng("dve", nc.vector)
        self.pool = Eng("pool", nc.gpsimd)
        self.sp = Eng("sp", nc.sync)
        self.engs = [self.pe, self.act, self.dve, self.pool, self.sp]
        self.dsems = []
        self.phase = 0
        self._new_sems()
        self.nres = 0

    def _new_sems(self):
        for e in self.engs:
            e.sem = self.es.enter_context(self.nc.semaphore(f"s_{e.name}{self.phase}"))
            e.count = 0
            e.waited = {}

    def dsem(self, name):
        d = DSem(self.es.enter_context(self.nc.semaphore(name)))
        self.dsems.append(d)
        return d

    def res(self, name="r"):
        self.nres += 1
        return Res(f"{name}{self.nres}")

    def fence(self, d):
        for e in self.engs:
            if d.count > 0 and e.waited.get(d.sem.num, 0) < d.count:
                e.h.wait_ge(d.sem, d.count)
                e.waited[d.sem.num] = d.count

    def barrier(self):
        for e in self.engs:
            for o in self.engs:
                if o is not e and o.count > 0:
                    e.h.wait_ge(o.sem, o.count)
            for d in self.dsems:
                if d.count > 0:
                    e.h.wait_ge(d.sem, d.count)
        self.phase += 1
        self._new_sems()


class Tile:
    def __init__(self, kb, t, name):
        self.t = t
        self.res = kb.res(name)

    def __getitem__(self, k):
        return self.t[k]


def sb(kb, es, name, shape, dt):
    return Tile(kb, es.enter_context(kb.nc.sbuf_tensor(name, list(shape), dt)), name)


class _VA:
    def __init__(self, tile, i):
        self.tile = tile
        self.i = i
        self.res = tile.res

    def __getitem__(self, k):
        return self.tile.t[:, self.i, :][k]


def bc3(ap, shape):
    return ap.unsqueeze(2).to_broadcast(list(shape))


def build(nc, S=S_FULL, dbg=False, phases=("A", "B", "C", "D", "E"), halves=2):
    NT = S // 128
    NTE = NT // halves
    SE = S // halves
    kind_s = "ExternalOutput" if dbg else "Internal"
    dram = {}

    def din(name, shape, dt=F32):
        dram[name] = nc.dram_tensor(name, list(shape), dt, kind="ExternalInput").ap()
        return dram[name]

    def dscr(name, shape, dt=F32):
        dram[name] = nc.dram_tensor(name, list(shape), dt, kind=kind_s).ap()
        return dram[name]

    x = din("x", [S, D])
    c = din("c", [D])
    w_ada = din("w_ada", [D, 6 * D])
    b_ada = din("b_ada", [6 * D])
    norm1_g = din("norm1_g", [D])
    w_in = din("w_in", [D, IN_COLS])
    mu_rwkv = din("mu_rwkv", [RWKV_COLS])
    w0 = din("w0", [RW])
    w2_decay = din("w2_decay", [64, RW])
    a0 = din("a0", [RW])
    a2_iclr = din("a2_iclr", [64, RW])
    g2_gate = din("g2_gate", [160, RW])
    k_k = din("k_k", [RW])
    k_a = din("k_a", [RW])
    r_k = din("r_k", [RW])
    lnx_w = din("lnx_w", [RW])
    lnx_b = din("lnx_b", [RW])
    q_norm_g = din("q_norm_g", [64])
    k_norm_g = din("k_norm_g", [64])
    w_branch = din("w_branch", [2, 512, D])
    w_out = din("w_out", [D, D])
    norm2_g = din("norm2_g", [D])
    w_peer_q = din("w_peer_q", [D, 2048])
    peer_sub_keys = din("peer_sub_keys", [2, 8, 128, 128])
    peer_u = din("peer_u", [16384, D])
    peer_v = din("peer_v", [16384, D])
    cmask = din("cmask", [128, 2048])
    cneg = din("cneg", [2, 32 * 32])
    cb_mask = din("cb_mask", [128, 4, 512], BF16)
    emat = din("emat", [32, S], BF16)
    tokidx = din("tokidx", [128, NTE], I32)
    out = nc.dram_tensor("out", [SE, D], F32, kind="ExternalOutput").ap()
    if dbg:
        dbg_x1 = nc.dram_tensor("dbg_x1", [SE, D], F32, kind="ExternalOutput").ap()
        dbg_idx = nc.dram_tensor("dbg_idx", [SE, 128], I32, kind="ExternalOutput").ap()
        dbg_gw = nc.dram_tensor("dbg_gw", [SE, 128], F32, kind="ExternalOutput").ap()

    s_rw = dscr("s_rw", [S, 6, RW])
    s_rk = dscr("s_rk", [S, 8])
    s_g = dscr("s_g", [S, RW])
    s_qt = dscr("s_qt", [8, 64, S], BF16)
    s_kt = dscr("s_kt", [8, 64, S], BF16)
    s_va = dscr("s_va", [S, 8, 65], BF16)
    s_gates = dscr("s_gates", [S, 2048], BF16)
    s_ya = dscr("s_ya", [S, RW], BF16)
    s_yb = dscr("s_yb", [S, RW], BF16)
    s_ada = dscr("s_ada", [1, 6 * D])
    s_uv = nc.dram_tensor("s_uv", [16384, 2 * D], BF16, kind="Internal").ap()

    with ExitStack() as es:
        kb = KB(nc, es)
        pe, act, dve, pool, sp = kb.pe, kb.act, kb.dve, kb.pool, kb.sp
        banks = [Tile(kb, es.enter_context(nc.psum_tensor(f"bank{i}", [128, 512], F32)), f"bank{i}") for i in range(8)]

        ident_f = sb(kb, es, "ident_f", [128, 128], F32)
        ident_b = sb(kb, es, "ident_b", [128, 128], BF16)
        ones_f = sb(kb, es, "ones_f", [128, 128], F32)
        ada_col = sb(kb, es, "ada_col", [128, 48], F32)
        eps_t = sb(kb, es, "eps_t", [128, 1], F32)
        g1_col = sb(kb, es, "g1_col", [128, 8], F32)
        gs1_col = sb(kb, es, "gs1_col", [128, 8], F32)
        d_c = kb.dsem("d_const")

        d_o = kb.dsem("d_misc_out")
        sp.dma(ident_f[:], cmask[:, 0:128], d_c, writes=[ident_f.res])
        kb.fence(d_c)
        dve.op("tensor_copy", out=ident_b[:], in_=ident_f[:], reads=[ident_f.res], writes=[ident_b.res])
        dve.op("memset", ones_f[:], 1.0, writes=[ones_f.res])
        dve.op("memset", eps_t[:], 1e-6, writes=[eps_t.res])

        def bcast_row(dst, row_ap, ncols, row_res, bank):
            for j in range(0, ncols, 512):
                w = min(512, ncols - j)
                pe.op("matmul", bank[:, 0:w], lhsT=ones_f[0:1, 0:128], rhs=row_ap[0:1, j:j + w], start=True, stop=True,
                      reads=[ones_f.res, row_res], writes=[bank.res])
                act.op("copy", out=dst[:, j:j + w], in_=bank[:, 0:w], reads=[bank.res], writes=[dst.res])

        with ExitStack() as ea:
            ada_row = sb(kb, ea, "ada_row", [1, 6 * D], F32)
            c_col = sb(kb, ea, "c_col", [128, 8], F32)
            sc_col = sb(kb, ea, "sc_col", [128, 8], F32)
            bada_row = sb(kb, ea, "bada_row", [1, 6 * D], F32)
            wa = [sb(kb, ea, f"wa{i}", [128, 8, 512], F32) for i in range(2)]
            d_wa = [kb.dsem(f"d_wa{i}") for i in range(2)]
            sp.dma(c_col[:], c.rearrange("(k p) -> p k", p=128), d_c, writes=[c_col.res], allow_slow_non_contiguous=True)
            sp.dma(g1_col[:], norm1_g.rearrange("(k p) -> p k", p=128), d_c, writes=[g1_col.res], allow_slow_non_contiguous=True)
            sp.dma(bada_row[:], b_ada.rearrange("(o n) -> o n", o=1), d_c, writes=[bada_row.res])
            kb.fence(d_c)
            act.op("activation", out=sc_col[:], in_=c_col[:], func=AF.Silu, reads=[c_col.res], writes=[sc_col.res])
            wa_v = w_ada.rearrange("(k p) n -> p k n", p=128)
            for cb in range(12):
                wt = wa[cb % 2]
                sp.dma(wt[:], wa_v[:, :, cb * 512:(cb + 1) * 512], d_wa[cb % 2], writes=[wt.res])
                bk = banks[cb % 2]
                for k in range(8):
                    pe.op("matmul", bk[0:1, 0:512], lhsT=sc_col[:, k:k + 1], rhs=wt[:, k, :], start=(k == 0), stop=(k == 7),
                          reads=[sc_col.res, wt.res], writes=[bk.res], signal=(k == 7))
                dve.op("tensor_tensor", out=ada_row[0:1, cb * 512:(cb + 1) * 512], in0=bk[0:1, 0:512],
                       in1=bada_row[0:1, cb * 512:(cb + 1) * 512], op=ALU.add,
                       reads=[bk.res, bada_row.res], writes=[ada_row.res])
            bk = banks[2]
            for m in range(48):
                pe.op("matmul", bk[:, m:m + 1], lhsT=ada_row[0:1, m * 128:(m + 1) * 128], rhs=ones_f[0:1, 0:1],
                      start=True, stop=True, reads=[ada_row.res, ones_f.res], writes=[bk.res], signal=(m == 47))
            dve.op("tensor_copy", out=ada_col[:], in_=bk[:, 0:48], reads=[bk.res], writes=[ada_col.res])
            dve.op("scalar_tensor_tensor", out=gs1_col[:], in0=ada_col[:, 8:16], scalar=1.0, in1=g1_col[:], op0=ALU.add, op1=ALU.mult,
                   reads=[ada_col.res, g1_col.res], writes=[gs1_col.res])
            sp.dma(s_ada[0:1, :], ada_row[:], d_o, reads=[ada_row.res])
            kb.barrier()


        if "B" in phases:
          with ExitStack() as eb:
            Wp = sb(kb, eb, "Wp", [128, 8, 7232], BF16)
            brow = sb(kb, eb, "brow", [1, 7232], BF16)
            with ExitStack() as eb0:
                bprime = sb(kb, eb0, "bprime", [1, IN_COLS], F32)
                mu_bc = sb(kb, eb0, "mu_bc", [128, RWKV_COLS], F32)
                om_bc = sb(kb, eb0, "om_bc", [128, RWKV_COLS], F32)
                w32 = [sb(kb, eb0, f"w32_{i}", [128, 8, 256], F32) for i in range(2)]
                d_w32 = [kb.dsem(f"d_w32_{i}") for i in range(2)]
                sp.dma(mu_bc[:], mu_rwkv.partition_broadcast(128), d_c, writes=[mu_bc.res])
                kb.fence(d_c)
                dve.op("tensor_scalar", out=om_bc[:], in0=mu_bc[:], scalar1=-1.0, scalar2=1.0, op0=ALU.mult, op1=ALU.add,
                       reads=[mu_bc.res], writes=[om_bc.res])
                win_v = w_in.rearrange("(k p) n -> p k n", p=128)
                for cb in range(22):
                    c0 = cb * 256
                    c1 = min(IN_COLS, c0 + 256)
                    w = c1 - c0
                    wt = w32[cb % 2]
                    sp.dma(wt[:, :, 0:w], win_v[:, :, c0:c1], d_w32[cb % 2], writes=[wt.res])
                    bk = banks[cb % 2]
                    for k in range(8):
                        pe.op("matmul", bk[0:1, 0:w], lhsT=ada_col[:, k:k + 1], rhs=wt[:, k, 0:w], start=(k == 0), stop=(k == 7),
                              reads=[ada_col.res, wt.res], writes=[bk.res], signal=(k == 7))
                    dve.op("tensor_copy", out=bprime[0:1, c0:c1], in_=bk[0:1, 0:w], reads=[bk.res], writes=[bprime.res])
                    ra, rb = c0, min(c1, RWKV_COLS)
                    for k in range(8):
                        if rb > ra:
                            dve.op("scalar_tensor_tensor", out=Wp[:, k, ra:rb], in0=wt[:, k, ra - c0:rb - c0], scalar=gs1_col[:, k:k + 1],
                                   in1=om_bc[:, ra:rb], op0=ALU.mult, op1=ALU.mult, reads=[wt.res, gs1_col.res, om_bc.res], writes=[Wp.res])
                            dve.op("scalar_tensor_tensor", out=Wp[:, k, RWKV_COLS + ra:RWKV_COLS + rb], in0=wt[:, k, ra - c0:rb - c0],
                                   scalar=gs1_col[:, k:k + 1], in1=mu_bc[:, ra:rb], op0=ALU.mult, op1=ALU.mult,
                                   reads=[wt.res, gs1_col.res, mu_bc.res], writes=[Wp.res])
                        qa, qb = max(c0, RWKV_COLS), c1
                        if qb > qa:
                            dve.op("tensor_scalar", out=Wp[:, k, RWKV_COLS + qa:RWKV_COLS + qb], in0=wt[:, k, qa - c0:qb - c0],
                                   scalar1=gs1_col[:, k:k + 1], scalar2=None, op0=ALU.mult, reads=[wt.res, gs1_col.res], writes=[Wp.res])
                dve.op("tensor_tensor", out=brow[0:1, 0:RWKV_COLS], in0=bprime[0:1, 0:RWKV_COLS], in1=om_bc[0:1, :], op=ALU.mult,
                       reads=[bprime.res, om_bc.res], writes=[brow.res])
                dve.op("tensor_tensor", out=brow[0:1, RWKV_COLS:2 * RWKV_COLS], in0=bprime[0:1, 0:RWKV_COLS], in1=mu_bc[0:1, :], op=ALU.mult,
                       reads=[bprime.res, mu_bc.res], writes=[brow.res])
                dve.op("tensor_copy", out=brow[0:1, 2 * RWKV_COLS:7232], in_=bprime[0:1, RWKV_COLS:IN_COLS], reads=[bprime.res], writes=[brow.res])
                kb.barrier()

            xt = [sb(kb, eb, f"xt{i}", [128, D], F32) for i in range(2)]
            d_x = [kb.dsem(f"d_x{i}") for i in range(2)]
            junk = sb(kb, eb, "junk", [128, D], BF16)
            ss = sb(kb, eb, "ss", [128, 1], F32)
            rstd = sb(kb, eb, "rstd", [128, 1], F32)
            xn = sb(kb, eb, "xn", [128, D], BF16)
            xnT = [sb(kb, eb, f"xnT{i}", [128, 8, 129], BF16) for i in range(2)]
            ones_a = sb(kb, eb, "ones_a", [1, 129], BF16)
            ones_b = sb(kb, eb, "ones_b", [1, 129], BF16)
            kk_bc = sb(kb, eb, "kk_bc", [128, RW], F32)
            ka_bc = sb(kb, eb, "ka_bc", [128, RW], F32)
            rk_bc = sb(kb, eb, "rk_bc", [128, RW], F32)
            qg_bc = sb(kb, eb, "qg_bc", [128, 8, 64], F32)
            kg_bc = sb(kb, eb, "kg_bc", [128, 8, 64], F32)
            w0a0 = sb(kb, eb, "w0a0", [1, 2 * RW], F32)
            w2d = sb(kb, eb, "w2d", [64, RW], F32)
            a2i = sb(kb, eb, "a2i", [64, RW], F32)
            g2g = sb(kb, eb, "g2g", [128, 2, RW], F32)
            tw = sb(kb, eb, "tw", [64, 128], F32)
            alT = sb(kb, eb, "alT", [64, 128], F32)
            sg0 = sb(kb, eb, "sg0", [128, 128], F32)
            sg1 = sb(kb, eb, "sg1", [32, 128], F32)
            a_s = sb(kb, eb, "a_s", [128, RW], F32)
            g_s = [sb(kb, eb, f"g_s{i}", [128, RW], F32) for i in range(1)]
            kraw = sb(kb, eb, "kraw", [128, RW], F32)
            t1 = sb(kb, eb, "t1", [128, RW], F32)
            t2 = sb(kb, eb, "t2", [128, RW], F32)
            t3 = sb(kb, eb, "t3", [128, RW], F32)
            ssq = sb(kb, eb, "ssq", [128, 8], F32)
            rinv = sb(kb, eb, "rinv", [128, 8], F32)
            rw_out = [sb(kb, eb, f"rw_out{i}", [128, 6, RW], F32) for i in range(1)]
            rk_out = [sb(kb, eb, f"rk_out{i}", [128, 8], F32) for i in range(2)]
            qm = sb(kb, eb, "qm", [128, RW], F32)
            qn = sb(kb, eb, "qn", [128, RW], BF16)
            kn = sb(kb, eb, "kn", [128, RW], BF16)
            va = [sb(kb, eb, f"va{i}", [128, 8, 65], BF16) for i in range(2)]
            qT_s = [sb(kb, eb, f"qT_s{i}", [128, 4, 128], BF16) for i in range(2)]
            kT_s = [sb(kb, eb, f"kT_s{i}", [128, 4, 128], BF16) for i in range(2)]
            gates_o = [sb(kb, eb, f"gates_o{i}", [128, 2048], BF16) for i in range(1)]
            d_st = [kb.dsem(f"d_stB{i}") for i in range(2)]
            eps64 = sb(kb, eb, "eps64", [128, 1], F32)

            sp.dma(kk_bc[:], k_k.partition_broadcast(128), d_c, writes=[kk_bc.res])
            sp.dma(ka_bc[:], k_a.partition_broadcast(128), d_c, writes=[ka_bc.res])
            sp.dma(rk_bc[:], r_k.partition_broadcast(128), d_c, writes=[rk_bc.res])
            for h in range(8):
                sp.dma(qg_bc[:, h, :], q_norm_g.partition_broadcast(128), d_c, writes=[qg_bc.res])
                sp.dma(kg_bc[:, h, :], k_norm_g.partition_broadcast(128), d_c, writes=[kg_bc.res])
            sp.dma(w0a0[0:1, 0:RW], w0.rearrange("(o n) -> o n", o=1), d_c, writes=[w0a0.res])
            sp.dma(w0a0[0:1, RW:2 * RW], a0.rearrange("(o n) -> o n", o=1), d_c, writes=[w0a0.res])
            sp.dma(w2d[:], w2_decay[:, :], d_c, writes=[w2d.res])
            sp.dma(a2i[:], a2_iclr[:, :], d_c, writes=[a2i.res])
            sp.dma(g2g[:, 0, :], g2_gate[0:128, :], d_c, writes=[g2g.res])
            sp.dma(g2g[0:32, 1, :], g2_gate[128:160, :], d_c, writes=[g2g.res])
            kb.fence(d_c)
            dve.op("tensor_scalar", out=qg_bc[:], in0=qg_bc[:], scalar1=0.125, scalar2=None, op0=ALU.mult, reads=[qg_bc.res], writes=[qg_bc.res])
            dve.op("memset", ones_a[:], 1.0, writes=[ones_a.res])
            dve.op("memset", ones_b[:], 1.0, writes=[ones_b.res])
            dve.op("memset", ones_b[0:1, 0:1], 0.0, writes=[ones_b.res])
            dve.op("memset", eps64[:], 1e-6, writes=[eps64.res])
            for i in range(2):
                dve.op("memset", va[i][:, :, 64:65], 1.0, writes=[va[i].res])
            bankT = banks[0]
            bT = bankT[:].bitcast(BF16)

            def head_rstd(src_ap, src_res, scale, eps_tile):
                dve.op("tensor_tensor", out=t2[:], in0=src_ap, in1=src_ap, op=ALU.mult, reads=[src_res], writes=[t2.res])
                dve.op("tensor_reduce", out=ssq[:], in_=t2[:].rearrange("p (h d) -> p h d", h=8), axis=AX.X, op=ALU.add,
                       reads=[t2.res], writes=[ssq.res])
                if eps_tile is None:
                    act.op("activation", out=ssq[:], in_=ssq[:], func=AF.Sqrt, scale=scale, reads=[ssq.res], writes=[ssq.res])
                    dve.op("tensor_scalar", out=ssq[:], in0=ssq[:], scalar1=1e-12, scalar2=None, op0=ALU.max, reads=[ssq.res], writes=[ssq.res])
                else:
                    act.op("activation", out=ssq[:], in_=ssq[:], func=AF.Sqrt, scale=scale, bias=eps_tile[:, 0:1],
                           reads=[ssq.res, eps_tile.res], writes=[ssq.res])
                dve.op("reciprocal", out=rinv[:], in_=ssq[:], reads=[ssq.res], writes=[rinv.res])

            def load_x(tt):
                sp.dma(xt[tt % 2][:], x[tt * 128:(tt + 1) * 128, :], d_x[tt % 2], writes=[xt[tt % 2].res])

            load_x(0)
            for tt in range(NT):
                if tt + 1 < NT:
                    load_x(tt + 1)
                xc = xt[tt % 2]
                xT = xnT[tt % 2]
                xTp = xnT[(tt + 1) % 2]
                ro = rw_out[0]
                rko = rk_out[tt % 2]
                gs_ = g_s[0]
                tsl = slice(tt * 128, (tt + 1) * 128)
                dst = d_st[tt % 2]
                act.op("activation", out=junk[:], in_=xc[:], func=AF.Square, accum_out=ss[:], reads=[xc.res], writes=[junk.res, ss.res])
                act.op("activation", out=rstd[:], in_=ss[:], func=AF.Sqrt, scale=1.0 / D, bias=eps_t[:, 0:1],
                       reads=[ss.res, eps_t.res], writes=[rstd.res])
                dve.op("reciprocal", out=rstd[:], in_=rstd[:], reads=[rstd.res], writes=[rstd.res])
                act.op("activation", out=xn[:], in_=xc[:], func=AF.Copy, scale=rstd[:, 0:1], reads=[xc.res, rstd.res], writes=[xn.res])
                for k in range(8):
                    pe.op("transpose", out=bT[:, k * 128:(k + 1) * 128], in_=xn[:, k * 128:(k + 1) * 128], identity=ident_b[:],
                          reads=[xn.res, ident_b.res], writes=[bankT.res], signal=(k == 7))
                dve.op("tensor_copy", out=xT[:, :, 1:129], in_=bT[:, :].rearrange("p (k t) -> p k t", k=8), reads=[bankT.res], writes=[xT.res])
                if tt == 0:
                    dve.op("memset", xT[:, :, 0:1], 0.0, writes=[xT.res])
                else:
                    dve.op("tensor_copy", out=xT[:, :, 0:1], in_=xTp[:, :, 128:129], reads=[xTp.res], writes=[xT.res])
                osh = ones_b if tt == 0 else ones_a

                def proj_tm(bk, w, cA, shifted):
                    n = 17 if shifted else 8
                    i = 0
                    for k in range(8):
                        pe.op("matmul", bk[:, 0:w], lhsT=xT[:, k, 1:129], rhs=Wp[:, k, cA:cA + w], start=(i == 0), stop=False,
                              reads=[xT.res, Wp.res], writes=[bk.res], signal=False)
                        i += 1
                    if shifted:
                        for k in range(8):
                            pe.op("matmul", bk[:, 0:w], lhsT=xT[:, k, 0:128], rhs=Wp[:, k, RWKV_COLS + cA:RWKV_COLS + cA + w], start=False, stop=False,
                                  reads=[xT.res, Wp.res], writes=[bk.res], signal=False)
                        pe.op("matmul", bk[:, 0:w], lhsT=osh[0:1, 0:128], rhs=brow[0:1, RWKV_COLS + cA:RWKV_COLS + cA + w], start=False, stop=False,
                              reads=[osh.res, brow.res], writes=[bk.res], signal=False)
                    pe.op("matmul", bk[:, 0:w], lhsT=ones_a[0:1, 1:129], rhs=brow[0:1, cA:cA + w], start=False, stop=True,
                          reads=[ones_a.res, brow.res], writes=[bk.res], signal=True)

                def proj_fm(bk, col0, M, cA):
                    o = bk[0:M, col0:col0 + 128]
                    for k in range(8):
                        pe.op("matmul", o, lhsT=Wp[:, k, cA:cA + M], rhs=xT[:, k, 1:129], start=(k == 0), stop=False,
                              reads=[xT.res, Wp.res], writes=[bk.res], signal=False)
                    for k in range(8):
                        pe.op("matmul", o, lhsT=Wp[:, k, RWKV_COLS + cA:RWKV_COLS + cA + M], rhs=xT[:, k, 0:128], start=False, stop=False,
                              reads=[xT.res, Wp.res], writes=[bk.res], signal=False)
                    pe.op("matmul", o, lhsT=brow[0:1, RWKV_COLS + cA:RWKV_COLS + cA + M], rhs=osh[0:1, 0:128], start=False, stop=False,
                          reads=[osh.res, brow.res], writes=[bk.res], signal=False)
                    pe.op("matmul", o, lhsT=brow[0:1, cA:cA + M], rhs=ones_a[0:1, 1:129], start=False, stop=True,
                          reads=[ones_a.res, brow.res], writes=[bk.res], signal=True)

                b4 = banks[4]
                proj_fm(b4, 0, 64, 1536)
                proj_fm(b4, 128, 64, 1600)
                proj_fm(b4, 256, 128, 1664)
                proj_fm(b4, 384, 32, 1792)
                act.op("activation", out=tw[:], in_=b4[0:64, 0:128], func=AF.Tanh, reads=[b4.res], writes=[tw.res])
                act.op("activation", out=sg0[:], in_=b4[:, 256:384], func=AF.Sigmoid, reads=[b4.res], writes=[sg0.res])
                act.op("activation", out=sg1[:], in_=b4[0:32, 384:512], func=AF.Sigmoid, reads=[b4.res], writes=[sg1.res])
                dve.op("tensor_copy", out=alT[:], in_=b4[0:64, 128:256], reads=[b4.res], writes=[alT.res])
                proj_tm(banks[1], 512, 0, True)
                proj_tm(banks[2], 512, 512, True)
                proj_tm(banks[3], 512, 1024, True)
                pe.op("matmul", banks[5][:, :], lhsT=tw[:, :], rhs=w2d[:, :], start=True, stop=False, reads=[tw.res, w2d.res], writes=[banks[5].res], signal=False)
                pe.op("matmul", banks[5][:, :], lhsT=ones_f[0:1, 0:128], rhs=w0a0[0:1, 0:RW], start=False, stop=True,
                      reads=[ones_f.res, w0a0.res], writes=[banks[5].res])
                pe.op("matmul", banks[6][:, :], lhsT=alT[:, :], rhs=a2i[:, :], start=True, stop=False, reads=[alT.res, a2i.res], writes=[banks[6].res], signal=False)
                pe.op("matmul", banks[6][:, :], lhsT=ones_f[0:1, 0:128], rhs=w0a0[0:1, RW:2 * RW], start=False, stop=True,
                      reads=[ones_f.res, w0a0.res], writes=[banks[6].res])
                pe.op("matmul", banks[7][:, :], lhsT=sg0[:, :], rhs=g2g[:, 0, :], start=True, stop=False, reads=[sg0.res, g2g.res], writes=[banks[7].res], signal=False)
                pe.op("matmul", banks[7][:, :], lhsT=sg1[:, :], rhs=g2g[0:32, 1, :], start=False, stop=True, reads=[sg1.res, g2g.res], writes=[banks[7].res])
                act.op("copy", out=ro[:, 0, :], in_=banks[1][:, :], reads=[banks[1].res], writes=[ro.res])
                act.op("copy", out=kraw[:], in_=banks[2][:, :], reads=[banks[2].res], writes=[kraw.res])
                act.op("copy", out=ro[:, 3, :], in_=banks[3][:, :], reads=[banks[3].res], writes=[ro.res])
                act.op("activation", out=t3[:], in_=banks[5][:, :], func=AF.Sigmoid, reads=[banks[5].res], writes=[t3.res])
                dve.op("tensor_scalar", out=ro[:, 1, :], in0=t3[:], scalar1=-0.6065306597126334, scalar2=None, op0=ALU.mult, reads=[t3.res], writes=[ro.res])
                act.op("activation", out=a_s[:], in_=banks[6][:, :], func=AF.Sigmoid, reads=[banks[6].res], writes=[a_s.res])
                act.op("copy", out=gs_[:], in_=banks[7][:, :], reads=[banks[7].res], writes=[gs_.res])
                dve.op("tensor_tensor", out=t1[:], in0=kraw[:], in1=kk_bc[:], op=ALU.mult, reads=[kraw.res, kk_bc.res], writes=[t1.res])
                head_rstd(t1[:], t1.res, 1.0, None)
                dve.op("tensor_tensor", out=ro[:, 4, :].rearrange("p (h d) -> p h d", h=8), in0=t1[:].rearrange("p (h d) -> p h d", h=8),
                       in1=bc3(rinv[:], [128, 8, 64]), op=ALU.mult, reads=[t1.res, rinv.res], writes=[ro.res])
                dve.op("scalar_tensor_tensor", out=t2[:], in0=a_s[:], scalar=-1.0, in1=ka_bc[:], op0=ALU.add, op1=ALU.mult,
                       reads=[a_s.res, ka_bc.res], writes=[t2.res])
                dve.op("scalar_tensor_tensor", out=ro[:, 2, :], in0=t2[:], scalar=1.0, in1=kraw[:], op0=ALU.add, op1=ALU.mult,
                       reads=[t2.res, kraw.res], writes=[ro.res])
                dve.op("tensor_tensor", out=ro[:, 5, :], in0=ro[:, 4, :], in1=a_s[:], op=ALU.mult, reads=[ro.res, a_s.res], writes=[ro.res])
                dve.op("tensor_tensor", out=t1[:], in0=ro[:, 0, :], in1=ro[:, 2, :], op=ALU.mult, reads=[ro.res], writes=[t1.res])
                dve.op("tensor_tensor", out=t1[:], in0=t1[:], in1=rk_bc[:], op=ALU.mult, reads=[t1.res, rk_bc.res], writes=[t1.res])
                dve.op("tensor_reduce", out=rko[:], in_=t1[:].rearrange("p (h d) -> p h d", h=8), axis=AX.X, op=ALU.add, reads=[t1.res], writes=[rko.res])
                sp.dma(s_rw[tsl, :, :], ro[:], dst, reads=[ro.res])
                sp.dma(s_rk[tsl, :], rko[:], dst, reads=[rko.res])
                sp.dma(s_g[tsl, :], gs_[:], dst, reads=[gs_.res])
                vo = va[tt % 2]
                qTs = qT_s[tt % 2]
                kTs = kT_s[tt % 2]
                proj_tm(banks[1], 512, 2 * RWKV_COLS, False)
                proj_tm(banks[2], 512, 2 * RWKV_COLS + 512, False)
                proj_tm(banks[3], 512, 2 * RWKV_COLS + 1024, False)
                act.op("copy", out=vo[:, :, 0:64], in_=banks[3][:, :].rearrange("p (h d) -> p h d", h=8), reads=[banks[3].res], writes=[vo.res])
                for (bk, gb, dstn, dT, sT) in ((banks[1], qg_bc, qn, qTs, s_qt), (banks[2], kg_bc, kn, kTs, s_kt)):
                    act.op("copy", out=qm[:], in_=bk[:, :], reads=[bk.res], writes=[qm.res])
                    head_rstd(qm[:], qm.res, 1.0 / 64, eps64)
                    dve.op("tensor_tensor", out=t1[:].rearrange("p (h d) -> p h d", h=8), in0=qm[:].rearrange("p (h d) -> p h d", h=8),
                           in1=bc3(rinv[:], [128, 8, 64]), op=ALU.mult, reads=[qm.res, rinv.res], writes=[t1.res])
                    dve.op("tensor_tensor", out=dstn[:], in0=t1[:], in1=gb[:].rearrange("p h d -> p (h d)"), op=ALU.mult,
                           reads=[t1.res, gb.res], writes=[dstn.res])
                    for j in range(4):
                        pe.op("transpose", out=bT[:, j * 128:(j + 1) * 128], in_=dstn[:, j * 128:(j + 1) * 128], identity=ident_b[:],
                              reads=[dstn.res, ident_b.res], writes=[bankT.res], signal=(j == 3))
                    dve.op("tensor_copy", out=dT[:], in_=bT[:, 0:512].rearrange("p (j t) -> p j t", j=4), reads=[bankT.res], writes=[dT.res])
                    sp.dma(sT[:, :, tsl].rearrange("(j h2) d t -> (h2 d) j t", h2=2), dT[:], dst, reads=[dT.res])
                sp.dma(s_va[tsl, :, :], vo[:], dst, reads=[vo.res])
                go = gates_o[0]
                for j in range(4):
                    bk = banks[5 + (j % 3)]
                    proj_tm(bk, 512, 2 * RWKV_COLS + 1536 + j * 512, False)
                    act.op("activation", out=go[:, j * 512:(j + 1) * 512], in_=bk[:, :], func=AF.Sigmoid, reads=[bk.res], writes=[go.res])
                sp.dma(s_gates[tsl, :], go[:], dst, reads=[go.res])
            kb.barrier()


        if "C" in phases:
          with ExitStack() as ec:
            NCH = S // 64
            NIB = 4
            rr = [0]

            def nb():
                rr[0] = (rr[0] + 1) % 8
                return banks[rr[0]]

            def t64(name, dt=F32):
                return sb(kb, ec, name, [64, RW], dt)

            SLm, SLTm, LETm, I8 = t64("SLm"), t64("SLTm"), t64("LETm"), t64("I8")
            Ltri = sb(kb, ec, "Ltri", [64, 64], F32)
            lnw_bc, lnb_bc = t64("lnw_bc"), t64("lnb_bc")
            eps_ln = sb(kb, ec, "eps_ln", [64, 1], F32)
            ST = t64("ST")
            inp = [sb(kb, ec, f"c_in{i}", [64, 6, RW], F32) for i in range(NIB)]
            gin = [t64(f"c_g{i}") for i in range(NIB)]
            rkin = [sb(kb, ec, f"c_rk{i}", [64, 8], F32) for i in range(NIB)]
            d_ci = [kb.dsem(f"d_ci{i}") for i in range(NIB)]
            d_co = [kb.dsem(f"d_co{i}") for i in range(2)]
            ya_o = [t64(f"ya_o{i}", BF16) for i in range(2)]

            class Ctx:
                pass

            ctxs = []
            for ci_ in range(2):
                c_ = Ctx()
                sfx = f"_{ci_}"
                for nm in ("cum_s", "cump", "tmc", "e_cum", "e_ncum", "e_cump", "e_tc", "e_tot", "At", "Bt", "Kt", "Rt", "Bh", "Kh", "Dg",
                           "AtT", "BtT", "KtT", "RtT", "Pm0", "Pm1", "PTm0", "PTm1", "MakT", "NrbT", "NrkT", "Q0", "Q1"):
                    setattr(c_, nm, t64(nm + sfx))
                c_.mean8 = sb(kb, ec, "mean8" + sfx, [64, 8], F32)
                c_.var8 = sb(kb, ec, "var8" + sfx, [64, 8], F32)
                c_.X2, c_.What, c_.Ahat, c_.PhiT, c_.RhT, c_.Osb, c_.cen, c_.sqv = (c_.cum_s, c_.cump, c_.tmc, c_.e_cum, c_.e_ncum, c_.e_cump, c_.e_tc, c_.e_tot)
                ctxs.append(c_)

            sp.dma(SLm[:], cmask[0:64, 512:1024], d_c, writes=[SLm.res])
            sp.dma(SLTm[:], cmask[0:64, 1024:1536], d_c, writes=[SLTm.res])
            sp.dma(LETm[:], cmask[0:64, 1536:2048], d_c, writes=[LETm.res])
            sp.dma(Ltri[:], cmask[0:64, 128:192], d_c, writes=[Ltri.res])
            sp.dma(lnw_bc[:], lnx_w.partition_broadcast(64), d_c, writes=[lnw_bc.res])
            sp.dma(lnb_bc[:], lnx_b.partition_broadcast(64), d_c, writes=[lnb_bc.res])
            kb.fence(d_c)
            for h in range(8):
                dve.op("tensor_copy", out=I8[:, h * 64:(h + 1) * 64], in_=ident_f[0:64, 0:64], reads=[ident_f.res], writes=[I8.res])
            dve.op("memset", ST[:], 0.0, writes=[ST.res])
            dve.op("memset", eps_ln[:], 64e-5, writes=[eps_ln.res])
            I64 = ident_f[0:64, 0:64]

            def hs(h):
                return slice(h * 64, (h + 1) * 64)

            def v3(ap):
                return ap.rearrange("p (h d) -> p h d", h=8)

            def load_c(ci):
                b_ = ci % NIB
                tsl = slice(ci * 64, (ci + 1) * 64)
                sp.dma(inp[b_][:], s_rw[tsl, :, :], d_ci[b_], writes=[inp[b_].res])
                sp.dma(gin[b_][:], s_g[tsl, :], d_ci[b_], writes=[gin[b_].res])
                sp.dma(rkin[b_][:], s_rk[tsl, :], d_ci[b_], writes=[rkin[b_].res])

            def mm_heads(bk, lhs, rhs, extra=None, signal_last=True):
                for h in range(8):
                    terms = [(lhs, rhs)] + (extra or [])
                    for ti, (l_, r_) in enumerate(terms):
                        l_ap = l_[0][:, hs(h)] if isinstance(l_, tuple) else I64
                        l_res = l_[0].res if isinstance(l_, tuple) else ident_f.res
                        pe.op("matmul", bk[0:64, hs(h)], lhsT=l_ap, rhs=r_[:, hs(h)], start=(ti == 0), stop=(ti == len(terms) - 1),
                              reads=[l_res, r_.res], writes=[bk.res], signal=(signal_last and h == 7 and ti == len(terms) - 1))

            def chunk_gen(ci, c):
                b_ = ci % NIB
                X = inp[b_]
                tsl = slice(ci * 64, (ci + 1) * 64)
                r_, ld_, k2_, v_, kkn_, bb_ = (X[:, i, :] for i in range(6))
                VV = _VA(X, 3)
                bcum, btot = nb(), nb()
                pe.op("matmul", bcum[0:64, :], lhsT=Ltri[:, :], rhs=ld_, start=True, stop=True, reads=[Ltri.res, X.res], writes=[bcum.res])
                pe.op("matmul", btot[0:64, :], lhsT=ones_f[0:64, 0:64], rhs=ld_, start=True, stop=True, reads=[ones_f.res, X.res], writes=[btot.res])
                yield
                act.op("copy", out=c.cum_s[:], in_=bcum[0:64, :], reads=[bcum.res], writes=[c.cum_s.res])
                act.op("activation", out=c.e_tot[:], in_=btot[0:64, :], func=AF.Exp, reads=[btot.res], writes=[c.e_tot.res])
                yield
                dve.op("tensor_tensor", out=c.cump[:], in0=c.cum_s[:], in1=ld_, op=ALU.subtract, reads=[c.cum_s.res, X.res], writes=[c.cump.res])
                dve.op("tensor_tensor", out=c.tmc[:], in0=btot[0:64, :], in1=c.cum_s[:], op=ALU.subtract, reads=[btot.res, c.cum_s.res], writes=[c.tmc.res])
                act.op("activation", out=c.e_cum[:], in_=c.cum_s[:], func=AF.Exp, reads=[c.cum_s.res], writes=[c.e_cum.res])
                act.op("activation", out=c.e_ncum[:], in_=c.cum_s[:], func=AF.Exp, scale=-1.0, reads=[c.cum_s.res], writes=[c.e_ncum.res])
                yield
                act.op("activation", out=c.e_cump[:], in_=c.cump[:], func=AF.Exp, reads=[c.cump.res], writes=[c.e_cump.res])
                act.op("activation", out=c.e_tc[:], in_=c.tmc[:], func=AF.Exp, reads=[c.tmc.res], writes=[c.e_tc.res])
                dve.op("tensor_tensor", out=c.Bt[:], in0=bb_, in1=c.e_ncum[:], op=ALU.mult, reads=[X.res, c.e_ncum.res], writes=[c.Bt.res])
                dve.op("tensor_tensor", out=c.Kt[:], in0=k2_, in1=c.e_ncum[:], op=ALU.mult, reads=[X.res, c.e_ncum.res], writes=[c.Kt.res])
                dve.op("tensor_tensor", out=c.Rt[:], in0=r_, in1=c.e_cum[:], op=ALU.mult, reads=[X.res, c.e_cum.res], writes=[c.Rt.res])
                yield
                dve.op("scalar_tensor_tensor", out=c.At[:], in0=kkn_, scalar=-1.0, in1=c.e_cump[:], op0=ALU.mult, op1=ALU.mult,
                       reads=[X.res, c.e_cump.res], writes=[c.At.res])
                dve.op("tensor_tensor", out=c.Bh[:], in0=bb_, in1=c.e_tc[:], op=ALU.mult, reads=[X.res, c.e_tc.res], writes=[c.Bh.res])
                dve.op("tensor_tensor", out=c.Kh[:], in0=k2_, in1=c.e_tc[:], op=ALU.mult, reads=[X.res, c.e_tc.res], writes=[c.Kh.res])
                dve.op("tensor_tensor", out=c.Dg[:], in0=c.e_tot[:], in1=I8[:], op=ALU.mult, reads=[c.e_tot.res, I8.res], writes=[c.Dg.res])
                for (src, dstT, eng) in ((c.Bt, c.BtT, dve), (c.Kt, c.KtT, act), (c.Rt, c.RtT, dve), (c.At, c.AtT, act)):
                    bk = nb()
                    for h in range(8):
                        pe.op("transpose", out=bk[0:64, hs(h)], in_=src[:, hs(h)], identity=I64, reads=[src.res, ident_f.res], writes=[bk.res], signal=(h == 7))
                    yield
                    if eng is act:
                        act.op("copy", out=dstT[:], in_=bk[0:64, :], reads=[bk.res], writes=[dstT.res])
                    else:
                        dve.op("tensor_copy", out=dstT[:], in_=bk[0:64, :], reads=[bk.res], writes=[dstT.res])
                yield
                P, PT = c.Pm0, c.PTm0
                for (lhs, rhs, dst_, msk) in ((c.AtT, c.BtT, P, SLm), (c.BtT, c.AtT, PT, SLTm), (c.KtT, c.AtT, c.MakT, SLTm), (c.BtT, c.RtT, c.NrbT, LETm), (c.KtT, c.RtT, c.NrkT, LETm)):
                    bk = nb()
                    mm_heads(bk, (lhs,), rhs)
                    yield
                    dve.op("tensor_tensor", out=dst_[:], in0=bk[0:64, :], in1=msk[:], op=ALU.mult, reads=[bk.res, msk.res], writes=[dst_.res])
                yield
                Qs = [c.Q0, c.Q1]
                Pms = [c.Pm0, c.Pm1]
                PTms = [c.PTm0, c.PTm1]
                Qc = Qs[0]
                dve.op("tensor_tensor", out=Qc[:], in0=PT[:], in1=I8[:], op=ALU.add, reads=[PT.res, I8.res], writes=[Qc.res])
                for lvl in range(5):
                    Pn, PTn = Pms[(lvl + 1) % 2], PTms[(lvl + 1) % 2]
                    bk = nb()
                    mm_heads(bk, (PT,), P)
                    if lvl < 4:
                        bk2 = nb()
                        mm_heads(bk2, (P,), PT)
                    yield
                    act.op("copy", out=Pn[:], in_=bk[0:64, :], reads=[bk.res], writes=[Pn.res])
                    if lvl < 4:
                        dve.op("tensor_copy", out=PTn[:], in_=bk2[0:64, :], reads=[bk2.res], writes=[PTn.res])
                    yield
                    Qn = Qs[(lvl + 1) % 2]
                    bk3 = nb()
                    mm_heads(bk3, (Pn,), Qc, extra=[(None, Qc)])
                    yield
                    act.op("copy", out=Qn[:], in_=bk3[0:64, :], reads=[bk3.res], writes=[Qn.res])
                    P, PT, Qc = Pn, PTn, Qn
                TT = Qc
                yield
                bk = nb()
                mm_heads(bk, (c.MakT,), VV)
                bka = nb()
                mm_heads(bka, (TT,), c.At)
                yield
                act.op("copy", out=c.X2[:], in_=bk[0:64, :], reads=[bk.res], writes=[c.X2.res])
                dve.op("tensor_copy", out=c.Ahat[:], in_=bka[0:64, :], reads=[bka.res], writes=[c.Ahat.res])
                yield
                bk = nb()
                mm_heads(bk, (TT,), c.X2)
                bkp = nb()
                mm_heads(bkp, (c.Ahat,), c.Bh, extra=[(None, c.Dg)])
                bkr = nb()
                mm_heads(bkr, (c.Ahat,), c.NrbT, extra=[(None, c.RtT)])
                yield
                dve.op("tensor_copy", out=c.What[:], in_=bk[0:64, :], reads=[bk.res], writes=[c.What.res])
                act.op("copy", out=c.PhiT[:], in_=bkp[0:64, :], reads=[bkp.res], writes=[c.PhiT.res])
                act.op("copy", out=c.RhT[:], in_=bkr[0:64, :], reads=[bkr.res], writes=[c.RhT.res])
                yield
                bko = nb()
                mm_heads(bko, (c.NrbT,), c.What, extra=[((c.NrkT,), VV), ((c.RhT,), ST)])
                bks = nb()
                mm_heads(bks, (c.Bh,), c.What, extra=[((c.Kh,), VV), ((c.PhiT,), ST)])
                act.op("copy", out=ST[:], in_=bks[0:64, :], reads=[bks.res], writes=[ST.res])
                yield
                yo = ya_o[ci % 2]
                Osb, cen, sqv, mean8, var8 = c.Osb, c.cen, c.sqv, c.mean8, c.var8
                dve.op("tensor_copy", out=Osb[:], in_=bko[0:64, :], reads=[bko.res], writes=[Osb.res])
                dve.op("tensor_reduce", out=mean8[:], in_=v3(Osb[:]), axis=AX.X, op=ALU.add, reads=[Osb.res], writes=[mean8.res])
                yield
                dve.op("tensor_scalar", out=mean8[:], in0=mean8[:], scalar1=1.0 / 64, scalar2=None, op0=ALU.mult, reads=[mean8.res], writes=[mean8.res])
                dve.op("tensor_tensor", out=v3(cen[:]), in0=v3(Osb[:]), in1=bc3(mean8[:], [64, 8, 64]), op=ALU.subtract,
                       reads=[Osb.res, mean8.res], writes=[cen.res])
                yield
                dve.op("tensor_tensor", out=sqv[:], in0=cen[:], in1=cen[:], op=ALU.mult, reads=[cen.res], writes=[sqv.res])
                dve.op("tensor_reduce", out=var8[:], in_=v3(sqv[:]), axis=AX.X, op=ALU.add, reads=[sqv.res], writes=[var8.res])
                yield
                act.op("activation", out=var8[:], in_=var8[:], func=AF.Sqrt, scale=1.0 / 64, bias=eps_ln[:, 0:1], reads=[var8.res, eps_ln.res], writes=[var8.res])
                yield
                dve.op("reciprocal", out=var8[:], in_=var8[:], reads=[var8.res], writes=[var8.res])
                dve.op("tensor_tensor", out=v3(cen[:]), in0=v3(cen[:]), in1=bc3(var8[:], [64, 8, 64]), op=ALU.mult, reads=[cen.res, var8.res], writes=[cen.res])
                yield
                dve.op("tensor_tensor", out=cen[:], in0=cen[:], in1=lnw_bc[:], op=ALU.mult, reads=[cen.res, lnw_bc.res], writes=[cen.res])
                dve.op("tensor_tensor", out=v3(sqv[:]), in0=v3(v_), in1=bc3(rkin[b_][:], [64, 8, 64]), op=ALU.mult, reads=[X.res, rkin[b_].res], writes=[sqv.res])
                yield
                dve.op("tensor_tensor", out=cen[:], in0=cen[:], in1=lnb_bc[:], op=ALU.add, reads=[cen.res, lnb_bc.res], writes=[cen.res])
                yield
                dve.op("tensor_tensor", out=cen[:], in0=cen[:], in1=sqv[:], op=ALU.add, reads=[cen.res, sqv.res], writes=[cen.res])
                yield
                dve.op("tensor_tensor", out=yo[:], in0=cen[:], in1=gin[b_][:], op=ALU.mult, reads=[cen.res, gin[b_].res], writes=[yo.res])
                sp.dma(s_ya[tsl, :], yo[:], d_co[ci % 2], reads=[yo.res])

            for ci in range(min(2, NCH)):
                load_c(ci)
            for pr in range(0, NCH, 2):
                for ci in (pr + 2, pr + 3):
                    if ci < NCH:
                        load_c(ci)
                gens = [chunk_gen(ci, ctxs[ci % 2]) for ci in (pr, pr + 1) if ci < NCH]
                if LOCKSTEP == 0:
                    for g_ in gens:
                        for _ in g_:
                            pass
                    gens = []
                while gens:
                    for g_ in list(gens):
                        try:
                            for _ in range(LOCKSTEP):
                                next(g_)
                        except StopIteration:
                            gens.remove(g_)
            kb.barrier()

        def make_conv(esc):
            cld = [sb(kb, esc, f"cld{i}", [128, D], F32) for i in range(3)]
            cst2 = [sb(kb, esc, f"cst2_{i}", [128, D], BF16) for i in range(2)]
            d_cl = [kb.dsem(f"d_cl{i}") for i in range(3)]
            d_cs = kb.dsem("d_cs")

            def gen():
                dv = s_uv.rearrange("(p j) d -> p j d", p=128)
                n = 0
                for ti_, src_t in enumerate((peer_u, peer_v)):
                    sv = src_t.rearrange("(p j) d -> p j d", p=128)
                    for j in range(128):
                        lb = cld[n % 3]
                        sp.dma(lb[:], sv[:, j, :], d_cl[n % 3], writes=[lb.res])
                        gr = cst2[n % 2]
                        dve.op("tensor_copy", out=gr[:], in_=lb[:], reads=[lb.res], writes=[gr.res])
                        sp.dma(dv[:, j, ti_ * D:(ti_ + 1) * D], gr[:], d_cs, reads=[gr.res])
                        n += 1
                        yield
            return gen()

        if "D" in phases:
          with ExitStack() as ed:
            NB = S // 256
            NG = S // 512
            KTa = sb(kb, ed, "KTa", [96, S], BF16)
            QTa = sb(kb, ed, "QTa", [96, S], BF16)
            Vall = sb(kb, ed, "Vall", [128, NT, 8 * 65], BF16)
            CB = sb(kb, ed, "CB", [128, 4, 512], BF16)
            negb = sb(kb, ed, "negb", [128, 32, 32], F32)
            ownb = sb(kb, ed, "ownb", [128, 32, 32], F32)
            kmf = sb(kb, ed, "kmf", [96, 32], F32)
            kmb = sb(kb, ed, "kmb", [96, 32], BF16)
            gm = sb(kb, ed, "gm", [128, NT, 32], F32)
            gq = sb(kb, ed, "gq", [128, NT, 32], F32)
            b2 = sb(kb, ed, "b2", [128, NT, 32], BF16)
            mx = sb(kb, ed, "mx", [128, NT], F32)
            NPT = 4
            PT = [sb(kb, ed, f"PT{i}", [128, 512], BF16) for i in range(NPT)]
            OT = [sb(kb, ed, f"OT{i}", [65, 512], F32) for i in range(2)]
            rden = sb(kb, ed, "rden", [128, 4, 1], F32)
            yb_o = [sb(kb, ed, f"yb_o{i}", [128, 4, 64], BF16) for i in range(2)]
            d_kq = kb.dsem("d_kq")
            d_yb = [kb.dsem(f"d_yb{i}") for i in range(2)]
            conv = make_conv(ed)

            sp.dma(KTa[64:96, :], emat[:, :], d_c, writes=[KTa.res])
            sp.dma(CB[:], cb_mask[:, :, :], d_c, writes=[CB.res])
            sp.dma(negb[:], cneg[0].partition_broadcast(128), d_c, writes=[negb.res])
            sp.dma(ownb[:], cneg[1].partition_broadcast(128), d_c, writes=[ownb.res])
            sp.dma(Vall[:], s_va.rearrange("(tt p) h e -> p tt (h e)", p=128), d_c, writes=[Vall.res])
            kb.fence(d_c)
            dve.op("memset", kmf[:], 0.0, writes=[kmf.res])
            dve.op("memset", kmb[:], 0.0, writes=[kmb.res])
            bOs = [banks[3], banks[5]]
            bS = [banks[0], banks[1], banks[2], banks[4]]
            bY = banks[6]
            it = 0
            for h in range(8):
                sp.dma(KTa[0:64, :], s_kt[h], d_kq, writes=[KTa.res])
                sp.dma(QTa[0:64, :], s_qt[h], d_kq, writes=[QTa.res])
                kb.fence(d_kq)
                dve.op("tensor_reduce", out=kmf[0:64, 0:NB], in_=KTa[0:64, :].rearrange("p (n k) -> p n k", k=256), axis=AX.X, op=ALU.add,
                       reads=[KTa.res], writes=[kmf.res])
                dve.op("tensor_scalar", out=kmb[0:64, 0:NB], in0=kmf[0:64, 0:NB], scalar1=1.0 / 256, scalar2=None, op0=ALU.mult,
                       reads=[kmf.res], writes=[kmb.res])
                TPB = 16
                nbk = (NT + TPB - 1) // TPB
                for bi in range(nbk):
                    bk = banks[bi]
                    t_lo, t_hi = bi * TPB, min(NT, (bi + 1) * TPB)
                    for tt in range(t_lo, t_hi):
                        pe.op("matmul", bk[:, (tt - t_lo) * 32:(tt - t_lo + 1) * 32], lhsT=QTa[0:64, tt * 128:(tt + 1) * 128], rhs=kmb[0:64, 0:32],
                              start=True, stop=True, reads=[QTa.res, kmb.res], writes=[bk.res], signal=(tt == t_hi - 1))
                    nq = (t_hi - t_lo) // 2
                    dve.op("tensor_tensor", out=gm[:, t_lo:t_hi, :].rearrange("p (q two) n -> p q two n", two=2),
                           in0=bk[:, 0:(t_hi - t_lo) * 32].rearrange("p (q two n) -> p q two n", two=2, n=32),
                           in1=negb[:, t_lo // 2:t_lo // 2 + nq, :].unsqueeze(2).to_broadcast([128, nq, 2, 32]), op=ALU.add,
                           reads=[bk.res, negb.res], writes=[gm.res])
                cur = gm
                for rnd in range(2):
                    dve.op("tensor_reduce", out=mx[:], in_=cur[:], axis=AX.X, op=ALU.max, reads=[cur.res], writes=[mx.res])
                    dve.op("tensor_tensor", out=b2[:], in0=cur[:], in1=bc3(mx[:], [128, NT, 32]), op=ALU.is_equal, reads=[cur.res, mx.res], writes=[b2.res])
                    dve.op("scalar_tensor_tensor", out=gq[:], in0=b2[:], scalar=NEG, in1=cur[:], op0=ALU.mult, op1=ALU.add,
                           reads=[b2.res, cur.res], writes=[gq.res])
                    cur = gq
                dve.op("tensor_reduce", out=mx[:], in_=gq[:], axis=AX.X, op=ALU.max, reads=[gq.res], writes=[mx.res])
                dve.op("tensor_scalar", out=mx[:], in0=mx[:], scalar1=-1e29, scalar2=None, op0=ALU.max, reads=[mx.res], writes=[mx.res])
                dve.op("tensor_tensor", out=gq[:], in0=gm[:], in1=bc3(mx[:], [128, NT, 32]), op=ALU.is_ge, reads=[gm.res, mx.res], writes=[gq.res])
                dve.op("tensor_scalar", out=gq[:], in0=gq[:], scalar1=1.0, scalar2=1e30, op0=ALU.subtract, op1=ALU.mult, reads=[gq.res], writes=[gq.res])
                dve.op("tensor_tensor", out=b2[:].rearrange("p (q two) n -> p q two n", two=2), in0=gq[:].rearrange("p (q two) n -> p q two n", two=2),
                       in1=ownb[:, 0:NT // 2, :].unsqueeze(2).to_broadcast([128, NT // 2, 2, 32]), op=ALU.add, reads=[gq.res, ownb.res], writes=[b2.res])
                for t4 in range(0, NT, 4):
                    bk = banks[4 + (t4 // 4) % 2]
                    for j in range(4):
                        pe.op("matmul", bk[64:96, j * 128:(j + 1) * 128], lhsT=b2[:, t4 + j, :], rhs=ident_b[:, :], start=True, stop=True,
                              reads=[b2.res, ident_b.res], writes=[bk.res], signal=(j == 3))
                    act.op("copy", out=QTa[64:96, t4 * 128:(t4 + 4) * 128], in_=bk[64:96, :], reads=[bk.res], writes=[QTa.res])
                its = [(g, kt) for g in range(NG) for kt in range(4 * g + 4)]
                n_it = len(its)
                NBS = len(bS)

                def emit_qk(i):
                    g, kt = its[i]
                    bs = bS[i % NBS]
                    ksl = slice(kt * 128, (kt + 1) * 128)
                    diag = kt >= 4 * g
                    pe.op("matmul", bs[:, :], lhsT=KTa[0:96, ksl], rhs=QTa[0:96, g * 512:(g + 1) * 512], start=True, stop=not diag,
                          reads=[KTa.res, QTa.res], writes=[bs.res], signal=not diag)
                    if diag:
                        pe.op("matmul", bs[:, :], lhsT=ident_b[:, :], rhs=CB[:, kt - 4 * g, :], start=False, stop=True,
                              reads=[ident_b.res, CB.res], writes=[bs.res])

                def emit_exp(i):
                    act.op("activation", out=PT[i % NPT][:], in_=bS[i % NBS][:, :], func=AF.Exp, reads=[bS[i % NBS].res], writes=[PT[i % NPT].res])

                def emit_pv(i):
                    g, kt = its[i]
                    nkt = 4 * g + 4
                    bo = bOs[g % 2]
                    pe.op("matmul", bo[0:65, :], lhsT=Vall[:, kt, h * 65:(h + 1) * 65], rhs=PT[i % NPT][:], start=(kt == 0), stop=(kt == nkt - 1),
                          reads=[Vall.res, PT[i % NPT].res], writes=[bo.res], signal=(kt == nkt - 1))

                def epi_a(g):
                    dve.op("tensor_copy", out=OT[g % 2][:], in_=bOs[g % 2][0:65, :], reads=[bOs[g % 2].res], writes=[OT[g % 2].res])

                def epi_b(g):
                    ot = OT[g % 2]
                    for j in range(4):
                        pe.op("transpose", out=bY[:, j * 65:(j + 1) * 65], in_=ot[:, j * 128:(j + 1) * 128], identity=ident_f[0:65, 0:65],
                              reads=[ot.res, ident_f.res], writes=[bY.res], signal=(j == 3))
                    yo = yb_o[g % 2]
                    yv = bY[:, 0:260].rearrange("p (j e) -> p j e", j=4)
                    dve.op("reciprocal", out=rden[:], in_=yv[:, :, 64:65], reads=[bY.res], writes=[rden.res])
                    dve.op("tensor_tensor", out=yo[:], in0=yv[:, :, 0:64], in1=rden[:].to_broadcast([128, 4, 64]), op=ALU.mult,
                           reads=[bY.res, rden.res], writes=[yo.res])
                    sp.dma(s_yb[g * 512:(g + 1) * 512, h * 64:(h + 1) * 64].rearrange("(j p) d -> p j d", p=128), yo[:], d_yb[g % 2], reads=[yo.res])

                LA = 2
                pend = []
                for i in range(min(LA, n_it)):
                    emit_qk(i)
                for i in range(n_it):
                    emit_exp(i)
                    if i + LA < n_it:
                        emit_qk(i + LA)
                    emit_pv(i)
                    g, kt = its[i]
                    pend = [(c - 1, gg) for (c, gg) in pend]
                    for (c, gg) in pend:
                        if c == 0:
                            epi_b(gg)
                    pend = [(c, gg) for (c, gg) in pend if c > 0]
                    if kt == 4 * g + 3:
                        epi_a(g)
                        pend.append((3, g))
                        for _cv in range(2):
                            next(conv, None)
                for (c, gg) in pend:
                    epi_b(gg)
            for _ in conv:
                pass
            kb.barrier()


        if "E" in phases:
          with ExitStack() as ee:
            NGR = 12
            GK = 4
            Wb = sb(kb, ee, "Wb", [128, 8, D], BF16)
            Wo = sb(kb, ee, "Wo", [128, 8, D], BF16)
            Wq = sb(kb, ee, "Wq", [128, 8, 2048], BF16)
            skT = sb(kb, ee, "skT", [128, 16, 128], F32)
            bc_gate1 = sb(kb, ee, "bc_gate1", [128, D], F32)
            bc_gs2 = sb(kb, ee, "bc_gs2", [128, D], F32)
            bc_shift2 = sb(kb, ee, "bc_shift2", [128, D], F32)
            bc_gate2 = sb(kb, ee, "bc_gate2", [128, D], F32)
            d_gb = [kb.dsem(f"d_gb{i}") for i in range(NGR)]
            d_gg = [kb.dsem(f"d_gg{i}") for i in range(NGR)]
            grow = [sb(kb, ee, f"grow{i}", [128, 2 * D], BF16) for i in range(NGR)]
            dg = [sb(kb, ee, f"dg{i}", [128, 128], BF16) for i in range(4)]
            tok_t = sb(kb, ee, "tok_t", [128, NTE], I32)
            d_eg = [kb.dsem(f"d_eg{i}") for i in range(2)]
            xt = [sb(kb, ee, f"ext{i}", [128, D], F32) for i in range(1)]
            ya_t = [sb(kb, ee, f"ya_t{i}", [128, RW], BF16) for i in range(1)]
            yb_t = [sb(kb, ee, f"yb_t{i}", [128, RW], BF16) for i in range(1)]
            gt_t = [sb(kb, ee, f"gt_t{i}", [128, 2048], BF16) for i in range(1)]
            d_ei = [kb.dsem(f"d_ei{i}") for i in range(2)]
            d_eo = kb.dsem("d_eo")
            yT = sb(kb, ee, "yT", [128, 8, 128], BF16)
            m0 = sb(kb, ee, "m0", [128, D], F32)
            msum = sb(kb, ee, "msum", [128, D], BF16)
            msT = sb(kb, ee, "msT", [128, 8, 128], BF16)
            x1 = sb(kb, ee, "x1", [128, D], F32)
            h2 = sb(kb, ee, "h2", [128, D], F32)
            h2b = sb(kb, ee, "h2b", [128, D], BF16)
            h2T = sb(kb, ee, "h2T", [128, 8, 128], BF16)

            djunk = sb(kb, ee, "djunk", [128, D], BF16)
            qT = sb(kb, ee, "qT", [128, 16, 128], F32)
            s12 = sb(kb, ee, "s12", [128, 16, 128], F32)
            work = sb(kb, ee, "work", [128, 256], F32)
            topv = sb(kb, ee, "topv", [128, 16, 16], F32)
            topi = sb(kb, ee, "topi", [128, 16, 16], U32)
            topif = sb(kb, ee, "topif", [128, 16, 16], F32)
            sc = sb(kb, ee, "sc", [128, 8, 16], F32)
            ci = sb(kb, ee, "ci", [128, 8, 16], U32)
            cdv = sb(kb, ee, "cdv", [128, 8, 16], U32)
            cdf = sb(kb, ee, "cdf", [128, 8, 16], F32)
            cmf = sb(kb, ee, "cmf", [128, 8, 16], F32)
            e1 = sb(kb, ee, "e1", [128, 8, 16], F32)
            e2 = sb(kb, ee, "e2", [128, 8, 16], F32)
            idxf = sb(kb, ee, "idxf", [128, 128], F32)
            idx = sb(kb, ee, "idx", [128, 128], I32)
            iota16 = sb(kb, ee, "iota16", [128, 16], F32)
            ssum = sb(kb, ee, "ssum", [128, 8], F32)
            gw = sb(kb, ee, "gw", [128, 8, 16], F32)
            hu = sb(kb, ee, "hu", [128, 128], F32)
            actv = sb(kb, ee, "actv", [128, 128], F32)
            acc = m0
            ss2 = sb(kb, ee, "ss2", [128, 1], F32)
            rstd2 = sb(kb, ee, "rstd2", [128, 1], F32)
            bT = banks[0][:].bitcast(BF16)
            bankT = banks[0]

            ld_i = [0]
            class _G:
                def __init__(self, t_):
                    self.t = t_
                    self.res = t_.res

                def __getitem__(self, k):
                    return self.t.t[:].bitcast(F32)[k]

            gbuf = [_G(g_) for g_ in grow]
            NGB = NGR
            def stage_load(src_ap):
                i = ld_i[0] % NGB
                ld_i[0] += 1
                sp.dma(gbuf[i][:, :], src_ap, d_gb[i], writes=[gbuf[i].res])
                return gbuf[i]

            wb_v = w_branch.rearrange("b (k p) n -> p b k n", p=128)
            for bk_ in range(8):
                g_ = stage_load(wb_v[:, bk_ // 4, bk_ % 4, :])
                dve.op("tensor_copy", out=Wb[:, bk_, :], in_=g_[:, :], reads=[g_.res], writes=[Wb.res])
            wo_v = w_out.rearrange("(k p) n -> p k n", p=128)
            for k in range(8):
                g_ = stage_load(wo_v[:, k, :])
                act.op("copy", out=Wo[:, k, :], in_=g_[:, :], reads=[g_.res], writes=[Wo.res])
            wq_v = w_peer_q.rearrange("(k p) n -> p k n", p=128)
            for k in range(8):
                for hf in range(2):
                    g_ = stage_load(wq_v[:, k, hf * 1024:(hf + 1) * 1024])
                    if hf == 0:
                        dve.op("tensor_copy", out=Wq[:, k, 0:1024], in_=g_[:, :], reads=[g_.res], writes=[Wq.res])
                    else:
                        act.op("copy", out=Wq[:, k, 1024:2048], in_=g_[:, :], reads=[g_.res], writes=[Wq.res])
            for cc in range(16):
                hp, side = cc // 2, cc % 2
                i = ld_i[0] % NGB
                ld_i[0] += 1
                sp.dma(gbuf[i][:, 0:128], peer_sub_keys[side, hp], d_gb[i], writes=[gbuf[i].res])
                bk = banks[1 + cc % 2]
                pe.op("transpose", out=bk[:, 0:128], in_=gbuf[i][:, 0:128], identity=ident_f[:], reads=[gbuf[i].res, ident_f.res], writes=[bk.res])
                dve.op("tensor_copy", out=skT[:, cc, :], in_=bk[:, 0:128], reads=[bk.res], writes=[skT.res])
            sp.dma(bc_gate1[:], s_ada[0, 2 * D:3 * D].partition_broadcast(128), d_c, writes=[bc_gate1.res])
            sp.dma(bc_gs2[:], s_ada[0, 4 * D:5 * D].partition_broadcast(128), d_c, writes=[bc_gs2.res])
            sp.dma(bc_shift2[:], s_ada[0, 3 * D:4 * D].partition_broadcast(128), d_c, writes=[bc_shift2.res])
            sp.dma(bc_gate2[:], s_ada[0, 5 * D:6 * D].partition_broadcast(128), d_c, writes=[bc_gate2.res])
            sp.dma(h2[:], norm2_g.partition_broadcast(128), d_c, writes=[h2.res])
            sp.dma(tok_t[:], tokidx[:, :], d_c, writes=[tok_t.res])
            kb.fence(d_c)
            kb.fence(d_eo)
            dve.op("scalar_tensor_tensor", out=bc_gs2[:], in0=bc_gs2[:], scalar=1.0, in1=h2[:], op0=ALU.add, op1=ALU.mult,
                   reads=[bc_gs2.res, h2.res], writes=[bc_gs2.res])
            pool.op("iota", iota16[:], pattern=[[1, 16]], base=0, channel_multiplier=0, allow_small_or_imprecise_dtypes=True, writes=[iota16.res])

            def gather(dst_tile, table_ap, idx_ap, dsem_, use_idx=False):
                deps = [tok_t.res] + ([idx.res] if use_idx else [])
                pool._pre(deps, [dst_tile.res])
                ins = nc.gpsimd.indirect_dma_start(out=dst_tile[:, :], out_offset=None, in_=table_ap,
                                                   in_offset=bass.IndirectOffsetOnAxis(ap=idx_ap, axis=0))
                dsem_.count += 16
                ins.then_inc(dsem_.sem, 16)
                pool._post((dsem_.sem, dsem_.count, dsem_), deps, [dst_tile.res])

            def load_e(tt):
                ia = tok_t[:, tt:tt + 1]
                gather(xt[0], x[:, :], ia, d_eg[0])
                gather(ya_t[0], s_ya[:, :], ia, d_eg[0])
                gather(yb_t[0], s_yb[:, :], ia, d_eg[0])
                gather(gt_t[0], s_gates[:, :], ia, d_eg[0])

            def v4(t_, hp_dim=8):
                return t_[:].rearrange("p h (i j) -> p h i j", i=16)

            gi = [0]
            load_e(0)
            for tt in range(NTE):
                tsl = slice(tt * 128, (tt + 1) * 128)
                xc, yat, ybt, gtt = xt[0], ya_t[0], yb_t[0], gt_t[0]
                for j in range(4):
                    pe.op("transpose", out=bT[:, j * 128:(j + 1) * 128], in_=yat[:, j * 128:(j + 1) * 128], identity=ident_b[:],
                          reads=[yat.res, ident_b.res], writes=[bankT.res], signal=False)
                for j in range(4):
                    pe.op("transpose", out=bT[:, (4 + j) * 128:(5 + j) * 128], in_=ybt[:, j * 128:(j + 1) * 128], identity=ident_b[:],
                          reads=[ybt.res, ident_b.res], writes=[bankT.res], signal=(j == 3))
                dve.op("tensor_copy", out=yT[:], in_=bT[:, :].rearrange("p (k t) -> p k t", k=8), reads=[bankT.res], writes=[yT.res])
                for br in range(2):
                    for hf in range(2):
                        bk = banks[1 + br * 2 + hf]
                        for k in range(4):
                            pe.op("matmul", bk[:, :], lhsT=yT[:, br * 4 + k, :], rhs=Wb[:, br * 4 + k, hf * 512:(hf + 1) * 512], start=(k == 0), stop=(k == 3),
                                  reads=[yT.res, Wb.res], writes=[bk.res], signal=(k == 3))
                for hf in range(2):
                    dve.op("tensor_tensor", out=m0[:, hf * 512:(hf + 1) * 512], in0=banks[1 + hf][:, :], in1=gtt[:, hf * 512:(hf + 1) * 512], op=ALU.mult,
                           reads=[banks[1 + hf].res, gtt.res], writes=[m0.res])
                for hf in range(2):
                    dve.op("tensor_tensor", out=h2[:, hf * 512:(hf + 1) * 512], in0=banks[3 + hf][:, :], in1=gtt[:, 1024 + hf * 512:1024 + (hf + 1) * 512],
                           op=ALU.mult, reads=[banks[3 + hf].res, gtt.res], writes=[h2.res])
                dve.op("tensor_tensor", out=msum[:], in0=m0[:], in1=h2[:], op=ALU.add, reads=[m0.res, h2.res], writes=[msum.res])
                for k in range(8):
                    pe.op("transpose", out=bT[:, k * 128:(k + 1) * 128], in_=msum[:, k * 128:(k + 1) * 128], identity=ident_b[:],
                          reads=[msum.res, ident_b.res], writes=[bankT.res], signal=(k == 7))
                act.op("copy", out=msT[:], in_=bT[:, :].rearrange("p (k t) -> p k t", k=8), reads=[bankT.res], writes=[msT.res])
                for hf in range(2):
                    bk = banks[5 + hf]
                    for k in range(8):
                        pe.op("matmul", bk[:, :], lhsT=msT[:, k, :], rhs=Wo[:, k, hf * 512:(hf + 1) * 512], start=(k == 0), stop=(k == 7),
                              reads=[msT.res, Wo.res], writes=[bk.res], signal=(k == 7))
                    dve.op("tensor_tensor", out=x1[:, hf * 512:(hf + 1) * 512], in0=bk[:, :], in1=bc_gate1[:, hf * 512:(hf + 1) * 512], op=ALU.mult,
                           reads=[bk.res, bc_gate1.res], writes=[x1.res])
                dve.op("tensor_tensor", out=x1[:], in0=x1[:], in1=xc[:], op=ALU.add, reads=[x1.res, xc.res], writes=[x1.res])
                if tt + 1 < NTE:
                    load_e(tt + 1)
                act.op("activation", out=h2b[:], in_=x1[:], func=AF.Square, accum_out=ss2[:], reads=[x1.res], writes=[h2b.res, ss2.res])
                act.op("activation", out=rstd2[:], in_=ss2[:], func=AF.Sqrt, scale=1.0 / D, bias=eps_t[:, 0:1], reads=[ss2.res, eps_t.res], writes=[rstd2.res])
                dve.op("reciprocal", out=rstd2[:], in_=rstd2[:], reads=[rstd2.res], writes=[rstd2.res])
                dve.op("scalar_tensor_tensor", out=h2[:], in0=x1[:], scalar=rstd2[:, 0:1], in1=bc_gs2[:], op0=ALU.mult, op1=ALU.mult,
                       reads=[x1.res, rstd2.res, bc_gs2.res], writes=[h2.res])
                dve.op("tensor_tensor", out=h2[:], in0=h2[:], in1=bc_shift2[:], op=ALU.add, reads=[h2.res, bc_shift2.res], writes=[h2.res])
                act.op("copy", out=h2b[:], in_=h2[:], reads=[h2.res], writes=[h2b.res])
                for k in range(8):
                    pe.op("transpose", out=bT[:, k * 128:(k + 1) * 128], in_=h2b[:, k * 128:(k + 1) * 128], identity=ident_b[:],
                          reads=[h2b.res, ident_b.res], writes=[bankT.res], signal=(k == 7))
                dve.op("tensor_copy", out=h2T[:], in_=bT[:, :].rearrange("p (k t) -> p k t", k=8), reads=[bankT.res], writes=[h2T.res])
                for qg in range(4):
                    bk = banks[1 + qg]
                    for c4 in range(4):
                        cc = qg * 4 + c4
                        for k in range(8):
                            pe.op("matmul", bk[:, c4 * 128:(c4 + 1) * 128], lhsT=Wq[:, k, cc * 128:(cc + 1) * 128], rhs=h2T[:, k, :], start=(k == 0), stop=(k == 7),
                                  reads=[Wq.res, h2T.res], writes=[bk.res], signal=(k == 7 and c4 == 3))
                    if qg % 2 == 0:
                        act.op("copy", out=qT[:, qg * 4:(qg + 1) * 4, :], in_=bk[:, :].rearrange("p (c t) -> p c t", c=4), reads=[bk.res], writes=[qT.res])
                    else:
                        dve.op("tensor_copy", out=qT[:, qg * 4:(qg + 1) * 4, :], in_=bk[:, :].rearrange("p (c t) -> p c t", c=4), reads=[bk.res], writes=[qT.res])
                for qg in range(4):
                    bk = banks[5 + qg % 3]
                    for c4 in range(4):
                        cc = qg * 4 + c4
                        pe.op("matmul", bk[:, c4 * 128:(c4 + 1) * 128], lhsT=qT[:, cc, :], rhs=skT[:, cc, :], start=True, stop=True,
                              reads=[qT.res, skT.res], writes=[bk.res], signal=(c4 == 3))
                    if qg % 2 == 0:
                        act.op("copy", out=s12[:, qg * 4:(qg + 1) * 4, :], in_=bk[:, :].rearrange("p (c t) -> p c t", c=4), reads=[bk.res], writes=[s12.res])
                    else:
                        dve.op("tensor_copy", out=s12[:, qg * 4:(qg + 1) * 4, :], in_=bk[:, :].rearrange("p (c t) -> p c t", c=4), reads=[bk.res], writes=[s12.res])
                for cc in range(16):
                    dve.op("max", out=topv[:, cc, 0:8], in_=s12[:, cc, :], reads=[s12.res], writes=[topv.res])
                    dve.op("max_index", out=topi[:, cc, 0:8], in_max=topv[:, cc, 0:8], in_values=s12[:, cc, :], reads=[topv.res, s12.res], writes=[topi.res])
                    dve.op("match_replace", out=work[:, 0:128], in_to_replace=topv[:, cc, 0:8], in_values=s12[:, cc, :], imm_value=NEG,
                           reads=[topv.res, s12.res], writes=[work.res])
                    dve.op("max", out=topv[:, cc, 8:16], in_=work[:, 0:128], reads=[work.res], writes=[topv.res])
                    dve.op("max_index", out=topi[:, cc, 8:16], in_max=topv[:, cc, 8:16], in_values=work[:, 0:128], reads=[topv.res, work.res], writes=[topi.res])
                dve.op("tensor_copy", out=topif[:], in_=topi[:], reads=[topi.res], writes=[topif.res])
                tv = topv[:].rearrange("p (h s) i -> p h s i", s=2)
                tif = topif[:].rearrange("p (h s) i -> p h s i", s=2)
                dve.op("tensor_tensor", out=s12[:].rearrange("p (h a) (b j) -> p h (a b) j", a=2, j=16), in0=tv[:, :, 0, :].unsqueeze(3).to_broadcast([128, 8, 16, 16]),
                       in1=tv[:, :, 1, :].unsqueeze(2).to_broadcast([128, 8, 16, 16]), op=ALU.add, reads=[topv.res], writes=[s12.res])
                for hp in range(8):
                    dve.op("max", out=sc[:, hp, 0:8], in_=s12[:, 2 * hp:2 * hp + 2, :].rearrange("p a x -> p (a x)"), reads=[s12.res], writes=[sc.res])
                    dve.op("max_index", out=ci[:, hp, 0:8], in_max=sc[:, hp, 0:8], in_values=s12[:, 2 * hp:2 * hp + 2, :].rearrange("p a x -> p (a x)"), reads=[sc.res, s12.res], writes=[ci.res])
                    dve.op("match_replace", out=work[:], in_to_replace=sc[:, hp, 0:8], in_values=s12[:, 2 * hp:2 * hp + 2, :].rearrange("p a x -> p (a x)"), imm_value=NEG,
                           reads=[sc.res, s12.res], writes=[work.res])
                    dve.op("max", out=sc[:, hp, 8:16], in_=work[:], reads=[work.res], writes=[sc.res])
                    dve.op("max_index", out=ci[:, hp, 8:16], in_max=sc[:, hp, 8:16], in_values=work[:], reads=[sc.res, work.res], writes=[ci.res])
                dve.op("tensor_single_scalar", out=cdv[:], in_=ci[:], scalar=4, op=ALU.logical_shift_right, reads=[ci.res], writes=[cdv.res])
                dve.op("tensor_copy", out=cdf[:], in_=cdv[:], reads=[cdv.res], writes=[cdf.res])
                dve.op("tensor_single_scalar", out=cdv[:], in_=ci[:], scalar=15, op=ALU.bitwise_and, reads=[ci.res], writes=[cdv.res])
                dve.op("tensor_copy", out=cmf[:], in_=cdv[:], reads=[cdv.res], writes=[cmf.res])
                io_b = iota16[:].unsqueeze(1).unsqueeze(1).to_broadcast([128, 8, 16, 16])
                for (cf, side, edst) in ((cdf, 0, e1), (cmf, 1, e2)):
                    dve.op("tensor_tensor", out=qT[:].rearrange("p (h a) (b j) -> p h (a b) j", a=2, j=16), in0=cf[:].unsqueeze(3).to_broadcast([128, 8, 16, 16]), in1=io_b, op=ALU.is_equal,
                           reads=[cf.res, iota16.res], writes=[qT.res])
                    dve.op("tensor_tensor", out=qT[:].rearrange("p (h a) (b j) -> p h (a b) j", a=2, j=16), in0=qT[:].rearrange("p (h a) (b j) -> p h (a b) j", a=2, j=16), in1=tif[:, :, side, :].unsqueeze(2).to_broadcast([128, 8, 16, 16]), op=ALU.mult,
                           reads=[qT.res, topif.res], writes=[qT.res])
                    dve.op("tensor_reduce", out=edst[:], in_=qT[:].rearrange("p (h a) (b j) -> p h (a b) j", a=2, j=16), axis=AX.X, op=ALU.add, reads=[qT.res], writes=[edst.res])
                dve.op("scalar_tensor_tensor", out=idxf[:], in0=e1[:].rearrange("p h j -> p (h j)"), scalar=128.0, in1=e2[:].rearrange("p h j -> p (h j)"),
                       op0=ALU.mult, op1=ALU.add, reads=[e1.res, e2.res], writes=[idxf.res])
                dve.op("tensor_copy", out=idx[:], in_=idxf[:], reads=[idxf.res], writes=[idx.res])
                dve.op("tensor_tensor", out=gw[:], in0=sc[:], in1=sc[:, :, 0:1].to_broadcast([128, 8, 16]), op=ALU.subtract, reads=[sc.res], writes=[gw.res])
                act.op("activation", out=gw[:], in_=gw[:], func=AF.Exp, reads=[gw.res], writes=[gw.res])
                dve.op("tensor_reduce", out=ssum[:], in_=gw[:], axis=AX.X, op=ALU.add, reads=[gw.res], writes=[ssum.res])
                dve.op("reciprocal", out=ssum[:], in_=ssum[:], reads=[ssum.res], writes=[ssum.res])
                dve.op("tensor_tensor", out=gw[:], in0=gw[:], in1=bc3(ssum[:], [128, 8, 16]), op=ALU.mult, reads=[gw.res, ssum.res], writes=[gw.res])
                gwf = gw[:].rearrange("p h j -> p (h j)")
                for g8 in range(128 // GK):
                    ks = range(g8 * GK, (g8 + 1) * GK)
                    bufs = []
                    for k in ks:
                        i = gi[0] % NGR
                        gi[0] += 1
                        bufs.append(grow[i])
                        gather(grow[i], s_uv[:, :], idx[:, k:k + 1], d_gg[i], True)
                        dve.op("scalar_tensor_tensor", out=djunk[:], in0=grow[i][:, 0:D], scalar=1.0, in1=h2[:], op0=ALU.mult, op1=ALU.mult, accum_out=hu[:, k:k + 1],
                               reads=[grow[i].res, h2.res], writes=[djunk.res, hu.res])
                    gsl = slice(g8 * GK, (g8 + 1) * GK)
                    act.op("activation", out=actv[:, gsl], in_=hu[:, gsl], func=AF.Gelu, reads=[hu.res], writes=[actv.res])
                    dve.op("tensor_tensor", out=actv[:, gsl], in0=actv[:, gsl], in1=gwf[:, gsl], op=ALU.mult, reads=[actv.res, gw.res], writes=[actv.res])
                    for k, gb_ in zip(ks, bufs):
                        dgk = dg[k % 4]
                        act.op("activation", out=dgk[:], in_=ident_b[:], func=AF.Copy, scale=actv[:, k:k + 1], reads=[ident_b.res, actv.res], writes=[dgk.res])
                        for hf in range(2):
                            pe.op("matmul", banks[6 + hf][:, :], lhsT=dgk[:, :], rhs=gb_[:, D + hf * 512:D + (hf + 1) * 512], start=(k == 0), stop=(k == 127),
                                  reads=[dgk.res, gb_.res], writes=[banks[6 + hf].res], signal=(hf == 1))
                for hf in range(2):
                    dve.op("tensor_tensor", out=acc[:, hf * 512:(hf + 1) * 512], in0=banks[6 + hf][:, :], in1=bc_gate2[:, hf * 512:(hf + 1) * 512], op=ALU.mult,
                           reads=[banks[6 + hf].res, bc_gate2.res], writes=[acc.res])
                dve.op("tensor_tensor", out=acc[:], in0=acc[:], in1=x1[:], op=ALU.add, reads=[acc.res, x1.res], writes=[acc.res])
                sp.dma(out[tsl, :], acc[:], d_eo, reads=[acc.res])
                if dbg:
                    sp.dma(dbg_x1[tsl, :], x1[:], d_eo, reads=[x1.res])
                    sp.dma(dbg_idx[tsl, :], idx[:], d_eo, reads=[idx.res])
                    sp.dma(dbg_gw[tsl, :], gw[:].rearrange("p h j -> p (h j)"), d_eo, reads=[gw.res])
            kb.barrier()

        if dbg:
            dbg_a = nc.dram_tensor("dbg_a", [128, 48], F32, kind="ExternalOutput").ap()
            d_dbg = kb.dsem("d_dbg")
            sp.dma(dbg_a[:, 0:48], ada_col[:], d_dbg, reads=[ada_col.res])

        kb.barrier()
    return nc


def make_cmask():
    m = np.zeros((128, 2048), np.float32)
    m[:, 0:128] = np.eye(128, dtype=np.float32)
    let = np.triu(np.ones((64, 64), np.float32))
    slt = np.triu(np.ones((64, 64), np.float32), 1)
    m[0:64, 128:192] = let
    m[0:64, 512:1024] = np.tile(slt.T, (1, 8))
    m[0:64, 1024:1536] = np.tile(slt, (1, 8))
    m[0:64, 1536:2048] = np.tile(let, (1, 8))
    return m


def make_consts(S):
    import ml_dtypes
    cneg = np.zeros((2, 32, 32), np.float32)
    for qb in range(32):
        cneg[0, qb, qb:] = NEG
        cneg[1, qb, qb] = -NEG
    i = np.arange(128)[:, None]
    j = np.arange(512)[None, :]
    cb = np.stack([np.where(128 * dlt + i <= j, 0.0, NEG) for dlt in range(4)], axis=1).astype(np.float32)
    e = (np.arange(S)[None, :] // 256 == np.arange(32)[:, None]).astype(np.float32)
    return {"cneg": cneg.reshape(2, 1024), "cb_mask": cb.astype(ml_dtypes.bfloat16), "emat": e.astype(ml_dtypes.bfloat16)}


INPUT_NAMES = ["w_ada", "b_ada", "norm1_g", "w_in", "mu_rwkv", "w0", "w2_decay", "a0", "a2_iclr", "g2_gate", "k_k", "k_a",
               "r_k", "lnx_w", "lnx_b", "q_norm_g", "k_norm_g", "w_branch", "w_out", "norm2_g", "w_peer_q",
               "peer_sub_keys", "peer_u", "peer_v"]


def make_in_maps(inputs, cores, S=S_FULL, halves=2):
    shared = {k: np.ascontiguousarray(np.asarray(inputs[k], dtype=np.float32)) for k in INPUT_NAMES}
    shared["r_k"] = shared["r_k"].reshape(RW)
    shared["cmask"] = make_cmask()
    shared.update(make_consts(S))
    nte = S // 128 // halves
    maps = []
    for (b, hf) in cores:
        m = dict(shared)
        m["x"] = np.ascontiguousarray(np.asarray(inputs["x"][b, :S], dtype=np.float32))
        m["c"] = np.ascontiguousarray(np.asarray(inputs["c"][b], dtype=np.float32))
        m["tokidx"] = (hf * (S // halves) + np.arange(nte)[None, :] * 128 + np.arange(128)[:, None]).astype(np.int32)
        maps.append(m)
    return maps


def kernel(**inputs):
    nc = bass.Bass("TRN2", target_bir_lowering=False)
    build(nc)
    cores = [(b, hf) for b in range(4) for hf in range(2)]
    maps = make_in_maps(inputs, cores)
    res = run_bass_kernel_spmd(nc, maps, core_ids=list(range(8)))
    outs = [np.asarray(r["out"]) for r in res.results]
    return np.stack([np.concatenate([outs[2 * b], outs[2 * b + 1]], axis=0) for b in range(4)], axis=0)
```
